# Optimizing a Trainium2 kernel written in Bass

```python
import math
import jax, jax.numpy as jnp
from jax import lax
import numpy as np

D_MODEL = 1024
BATCH = 2
SEQ = 16384
DEPTH = 4

N_A = DEPTH // 2
N_B = DEPTH - N_A
MEM_TOKENS = 256
MEM_HEADS = 4
MEM_DH = D_MODEL // 16
MEM_W = MEM_HEADS * MEM_DH
DN_DK = 128
DN_DV = 128
DN_HEADS = (3 * D_MODEL) // (4 * DN_DV)
DN_QK_W = DN_HEADS * DN_DK
DN_V_W = DN_HEADS * DN_DV
CONV_WIDTH = 4
CHUNK = 64
SWA_DH = 64
SWA_HEADS = (3 * D_MODEL) // (4 * SWA_DH)
SWA_KV_HEADS = 2
SWA_GROUP = SWA_HEADS // SWA_KV_HEADS
SWA_Q_W = SWA_HEADS * SWA_DH
SWA_KV_W = SWA_KV_HEADS * SWA_DH
WINDOW = 128
ROPE_THETA = 10000.0
MLP_HIDDEN = 4 * D_MODEL
LN_EPS = 1e-5
NORM_EPS = 1e-6
DN_ALPHA = (2.0 * DEPTH) ** 0.25
DN_BETA = (8.0 * DEPTH) ** -0.25
A_IN = 2 * DN_QK_W + 2 * DN_V_W + 2 * DN_HEADS + MEM_W
B_IN = SWA_Q_W + MEM_W
MIX_W = DN_V_W + MEM_W

kernel_name = "yoco_deltanet_swa_sink_memory_trunk"


def layer_norm(x, g, b):
    xf = x.astype(jnp.float32)
    mu = jnp.mean(xf, axis=-1, keepdims=True)
    var = jnp.mean(jnp.square(xf - mu), axis=-1, keepdims=True)
    y = (xf - mu) * lax.rsqrt(var + LN_EPS) * g.astype(jnp.float32) + b.astype(jnp.float32)
    return y.astype(x.dtype)


def l2_normalize(x):
    xf = x.astype(jnp.float32)
    return xf * lax.rsqrt(jnp.sum(xf * xf, axis=-1, keepdims=True) + NORM_EPS)


def rope_tables(positions, dh):
    inv_freq = ROPE_THETA ** (-jnp.arange(0, dh, 2, dtype=jnp.float32) / dh)
    ang = positions.astype(jnp.float32)[..., None] * inv_freq
    return jnp.cos(ang)[:, :, None, :], jnp.sin(ang)[:, :, None, :]


def apply_rope(x, cos, sin):
    xf = x.astype(jnp.float32)
    x1, x2 = jnp.split(xf, 2, axis=-1)
    out = jnp.concatenate([x1 * cos - x2 * sin, x2 * cos + x1 * sin], axis=-1)
    return out.astype(x.dtype)


def causal_depthwise_conv(x, w):
    C = x.shape[-1]
    return lax.conv_general_dilated(
        x, w[:, None, :].astype(x.dtype), window_strides=(1,),
        padding=[(CONV_WIDTH - 1, 0)], dimension_numbers=('NWC', 'WIO', 'NWC'),
        feature_group_count=C)


def gated_delta_rule(q, k, v, g, beta):
    B, S, H, DK = q.shape
    DV = v.shape[-1]
    N = S // CHUNK
    f32 = jnp.float32

    def to_chunks(t):
        t = t.astype(f32).reshape((B, N, CHUNK, H) + t.shape[3:])
        return jnp.moveaxis(t, 3, 1)

    q = to_chunks(q) * (DK ** -0.5)
    k = to_chunks(k)
    v = to_chunks(v)
    beta = to_chunks(beta)
    g = jnp.cumsum(to_chunks(g), axis=-1)
    incl = jnp.tril(jnp.ones((CHUNK, CHUNK), bool))
    strict = jnp.tril(jnp.ones((CHUNK, CHUNK), bool), -1)
    decay = jnp.exp(jnp.where(incl, g[..., :, None] - g[..., None, :], -jnp.inf))
    k_beta = k * beta[..., None]
    L = jnp.where(strict, jnp.einsum('bhnid,bhnjd->bhnij', k_beta, k) * decay, 0.0)
    rhs = jnp.concatenate([v * beta[..., None], k_beta * jnp.exp(g)[..., None]], axis=-1)
    sol = lax.linalg.triangular_solve(L, rhs, left_side=True, lower=True, unit_diagonal=True)
    u, w = sol[..., :DV], sol[..., DV:]
    intra = jnp.where(incl, jnp.einsum('bhnid,bhnjd->bhnij', q, k) * decay, 0.0)
    q_dec = q * jnp.exp(g)[..., None]
    k_dec = k * jnp.exp(g[..., -1:] - g)[..., None]
    chunk_decay = jnp.exp(g[..., -1])

    def step(state, inp):
        qd, kd, uc, wc, ac, cd = inp
        v_new = uc - jnp.einsum('bhik,bhkv->bhiv', wc, state)
        out = jnp.einsum('bhik,bhkv->bhiv', qd, state) + jnp.einsum('bhij,bhjv->bhiv', ac, v_new)
        state = state * cd[..., None, None] + jnp.einsum('bhik,bhiv->bhkv', kd, v_new)
        return state, out

    xs = (jnp.moveaxis(q_dec, 2, 0), jnp.moveaxis(k_dec, 2, 0), jnp.moveaxis(u, 2, 0),
          jnp.moveaxis(w, 2, 0), jnp.moveaxis(intra, 2, 0), jnp.moveaxis(chunk_decay, 2, 0))
    s0 = jnp.zeros((B, H, DK, DV), f32)
    _, out = lax.scan(step, s0, xs)
    return jnp.transpose(out, (1, 0, 3, 2, 4)).reshape(B, S, H, DV)


def sliding_window_sink_attention(q, k, v, sinks):
    B, S, HKV, G, dh = q.shape
    NB = S // WINDOW
    qb = q.reshape(B, NB, WINDOW, HKV, G, dh)

    def band_keys(t):
        tb = t.reshape(B, NB, WINDOW, HKV, dh)
        prev = jnp.pad(tb, ((0, 0), (1, 0), (0, 0), (0, 0), (0, 0)))[:, :-1]
        return jnp.concatenate([prev, tb], axis=2)

    kk = band_keys(k)
    vv = band_keys(v)
    s = jnp.einsum('bnqhgd,bnkhd->bnhgqk', qb, kk).astype(jnp.float32) * (dh ** -0.5)
    qi = jnp.arange(WINDOW)[:, None]
    kj = jnp.arange(2 * WINDOW)[None, :]
    diff = qi + WINDOW - kj
    band = (diff >= 0) & (diff < WINDOW)
    key_pos = jnp.arange(NB)[:, None] * WINDOW - WINDOW + kj
    valid = band[None] & (key_pos >= 0)[:, None, :]
    s = jnp.where(valid[None, :, None, None], s, -jnp.inf)
    sink = sinks.astype(jnp.float32).reshape(HKV, G)[None, None, :, :, None, None]
    m = jnp.maximum(jnp.max(s, axis=-1, keepdims=True), sink)
    p = jnp.exp(s - m)
    denom = jnp.sum(p, axis=-1, keepdims=True) + jnp.exp(sink - m)
    probs = (p / denom).astype(v.dtype)
    o = jnp.einsum('bnhgqk,bnkhd->bnqhgd', probs, vv)
    return o.reshape(B, S, HKV * G * dh)


def memory_cross_attention(qm, mem, w_kv):
    B, S, _ = qm.shape
    M = mem.shape[1]
    kv = mem @ w_kv
    k = kv[..., :MEM_W].reshape(B, M, MEM_HEADS, MEM_DH)
    v = kv[..., MEM_W:].reshape(B, M, MEM_HEADS, MEM_DH)
    q = qm.reshape(B, S, MEM_HEADS, MEM_DH)
    s = jnp.einsum('bshd,bmhd->bhsm', q, k).astype(jnp.float32) * (MEM_DH ** -0.5)
    p = jax.nn.softmax(s, axis=-1).astype(v.dtype)
    return jnp.einsum('bhsm,bmhd->bshd', p, v).reshape(B, S, MEM_W)


def mixer_a(h, mem, w_in, conv_w, A_log, dt_bias, norm_w, mem_w_kv, w_o):
    B, S, _ = h.shape
    proj = h @ w_in
    c1 = 2 * DN_QK_W + DN_V_W
    qkv = proj[..., :c1]
    z = proj[..., c1:c1 + DN_V_W]
    a = proj[..., c1 + DN_V_W:c1 + DN_V_W + DN_HEADS]
    b = proj[..., c1 + DN_V_W + DN_HEADS:c1 + DN_V_W + 2 * DN_HEADS]
    qm = proj[..., c1 + DN_V_W + 2 * DN_HEADS:]
    qkv = jax.nn.silu(causal_depthwise_conv(qkv, conv_w))
    q = l2_normalize(qkv[..., :DN_QK_W].reshape(B, S, DN_HEADS, DN_DK))
    k = l2_normalize(qkv[..., DN_QK_W:2 * DN_QK_W].reshape(B, S, DN_HEADS, DN_DK))
    v = qkv[..., 2 * DN_QK_W:].reshape(B, S, DN_HEADS, DN_DV)
    beta = jax.nn.sigmoid(b.astype(jnp.float32))
    g = -jnp.exp(A_log.astype(jnp.float32)) * jax.nn.softplus(a.astype(jnp.float32) + dt_bias.astype(jnp.float32))
    o = gated_delta_rule(q, k, v, g, beta)
    o = o * lax.rsqrt(jnp.mean(o * o, axis=-1, keepdims=True) + NORM_EPS) * norm_w.astype(jnp.float32)
    o = o * jax.nn.silu(z.astype(jnp.float32).reshape(B, S, DN_HEADS, DN_DV))
    o = o.astype(h.dtype).reshape(B, S, DN_V_W)
    mo = memory_cross_attention(qm, mem, mem_w_kv)
    return jnp.concatenate([o, mo], axis=-1) @ w_o


def mixer_b(h, mem, k_sh, v_sh, cos, sin, w_in, sinks, mem_w_kv, w_o):
    B, S, _ = h.shape
    proj = h @ w_in
    q = apply_rope(proj[..., :SWA_Q_W].reshape(B, S, SWA_HEADS, SWA_DH), cos, sin)
    q = q.reshape(B, S, SWA_KV_HEADS, SWA_GROUP, SWA_DH)
    o = sliding_window_sink_attention(q, k_sh, v_sh, sinks)
    mo = memory_cross_attention(proj[..., SWA_Q_W:], mem, mem_w_kv)
    return jnp.concatenate([o, mo], axis=-1) @ w_o


def shared_kv(h, w_kv, cos, sin):
    B, S, _ = h.shape
    kv = h @ w_kv
    k = apply_rope(kv[..., :SWA_KV_W].reshape(B, S, SWA_KV_HEADS, SWA_DH), cos, sin)
    v = kv[..., SWA_KV_W:].reshape(B, S, SWA_KV_HEADS, SWA_DH)
    return k, v


def sq_relu_mlp(h, w_up, w_down):
    return jnp.square(jax.nn.relu(h @ w_up)) @ w_down


def setup_inputs(seed: int = 0) -> dict:
    key = jax.random.key(seed)
    ks = jax.random.split(key, 20)
    f32 = jnp.float32

    def dense(k, shape, fan_in, scale=1.0):
        return jax.random.normal(k, shape, f32) * (fan_in ** -0.5) * scale

    x = jax.random.normal(ks[0], (BATCH, SEQ, D_MODEL), f32)
    mem = jax.random.normal(ks[1], (BATCH, MEM_TOKENS, D_MODEL), f32)
    positions = (jax.random.randint(ks[2], (BATCH, 1), 0, 4096, jnp.int32)
                 + jnp.arange(SEQ, dtype=jnp.int32)[None, :])
    a_w_in = dense(ks[3], (N_A, D_MODEL, A_IN), D_MODEL)
    a_conv_w = jax.random.normal(ks[4], (N_A, CONV_WIDTH, 2 * DN_QK_W + DN_V_W), f32) * (CONV_WIDTH ** -0.5)
    a_A_log = jnp.log(jax.random.uniform(ks[5], (N_A, DN_HEADS), f32, 1.0, 16.0))
    dt = jnp.exp(jax.random.uniform(ks[6], (N_A, DN_HEADS), f32, math.log(1e-3), math.log(1e-1)))
    a_dt_bias = dt + jnp.log(-jnp.expm1(-dt))
    a_norm_w = 1.0 + 0.02 * jax.random.normal(ks[7], (N_A, DN_DV), f32)
    b_w_in = dense(ks[8], (N_B, D_MODEL, B_IN), D_MODEL)
    b_sinks = 0.5 * jax.random.normal(ks[9], (N_B, SWA_HEADS), f32)
    w_kv_shared = dense(ks[10], (D_MODEL, 2 * SWA_KV_W), D_MODEL)
    mem_w_kv = dense(ks[11], (DEPTH, D_MODEL, 2 * MEM_W), D_MODEL)
    w_o = dense(ks[12], (DEPTH, MIX_W, D_MODEL), MIX_W, DN_BETA)
    mlp_w_up = dense(ks[13], (DEPTH, D_MODEL, MLP_HIDDEN), D_MODEL)
    mlp_w_down = dense(ks[14], (DEPTH, MLP_HIDDEN, D_MODEL), MLP_HIDDEN, DN_BETA)
    ln_g = 1.0 + 0.02 * jax.random.normal(ks[15], (DEPTH, 2, D_MODEL), f32)
    ln_b = 0.02 * jax.random.normal(ks[16], (DEPTH, 2, D_MODEL), f32)
    return {"x": x, "mem": mem, "positions": positions, "a_w_in": a_w_in, "a_conv_w": a_conv_w,
            "a_A_log": a_A_log, "a_dt_bias": a_dt_bias, "a_norm_w": a_norm_w, "b_w_in": b_w_in,
            "b_sinks": b_sinks, "w_kv_shared": w_kv_shared, "mem_w_kv": mem_w_kv, "w_o": w_o,
            "mlp_w_up": mlp_w_up, "mlp_w_down": mlp_w_down, "ln_g": ln_g, "ln_b": ln_b}


def reference(x, mem, positions, a_w_in, a_conv_w, a_A_log, a_dt_bias, a_norm_w, b_w_in,
              b_sinks, w_kv_shared, mem_w_kv, w_o, mlp_w_up, mlp_w_down, ln_g, ln_b):
    cos, sin = rope_tables(positions, SWA_DH)
    h = x
    k_sh = None
    v_sh = None
    for layer in range(DEPTH):
        if layer < N_A:
            mix = mixer_a(h, mem, a_w_in[layer], a_conv_w[layer], a_A_log[layer], a_dt_bias[layer],
                          a_norm_w[layer], mem_w_kv[layer], w_o[layer])
        else:
            j = layer - N_A
            mix = mixer_b(h, mem, k_sh, v_sh, cos, sin, b_w_in[j], b_sinks[j], mem_w_kv[layer], w_o[layer])
        h = layer_norm(DN_ALPHA * h + mix, ln_g[layer, 0], ln_b[layer, 0])
        h = layer_norm(DN_ALPHA * h + sq_relu_mlp(h, mlp_w_up[layer], mlp_w_down[layer]),
                       ln_g[layer, 1], ln_b[layer, 1])
        if layer == N_A - 1:
            k_sh, v_sh = shared_kv(h, w_kv_shared, cos, sin)
    return h
```

```python
import contextlib
import numpy as np
import concourse.bass as bass
import concourse.mybir as mybir
from concourse.bass_utils import run_bass_kernel_spmd

F32 = mybir.dt.float32
BF16 = mybir.dt.bfloat16
I32 = mybir.dt.int32
AF = mybir.ActivationFunctionType
ALU = mybir.AluOpType
AX = mybir.AxisListType

SAME_SYNC = True


class Buf:
    def __init__(self, name, t=None):
        self.name = name
        self.t = t
        self.last_write = None
        self.readers = {}
        self.dsem = {}
        self.dcount = {}
        self.subs = {}
        self.excl = False

    def __getitem__(self, key):
        return self.t[key]

    def sub(self, key):
        if self.excl:
            return self
        if key not in self.subs:
            b = Buf(f"{self.name}.{key}", self.t)
            self.subs[key] = b
        return self.subs[key]


class K:
    ENG = ("pe", "act", "dve", "pool", "sp")

    def __init__(self, nc):
        self.nc = nc
        self.stack = contextlib.ExitStack()
        self.prog = {e: [] for e in self.ENG}
        self.count = {e: 0 for e in self.ENG}
        self.known = {e: {} for e in self.ENG}
        self.esem = {e: self.stack.enter_context(nc.semaphore("es_" + e)) for e in self.ENG}
        self.nsem = 0
        self.out_events = []
        self.all_dma = {}
        self.free_sems = {}
        self.scopes = []

    def new_sem(self, name):
        self.nsem += 1
        return self.stack.enter_context(self.nc.semaphore(f"ds{self.nsem}_{name}"))

    def push(self):
        st = contextlib.ExitStack()
        st.bufs = []
        self.scopes.append(st)

    def pop(self):
        self.barrier()
        st = self.scopes.pop()
        for b in st.bufs:
            for bb in [b] + list(b.subs.values()):
                for q, sm in bb.dsem.items():
                    self.free_sems.setdefault(q, []).append((sm, bb.dcount[q]))
                bb.dsem = {}
        st.close()

    def sbuf(self, name, shape, dtype):
        st = self.scopes[-1] if self.scopes else self.stack
        self.ntens = getattr(self, "ntens", 0) + 1
        t = st.enter_context(self.nc.sbuf_tensor(f"{name}_{self.ntens}", list(shape), dtype))
        b = Buf(name, t)
        if self.scopes:
            st.bufs.append(b)
        return b

    def psum(self, name, shape, dtype):
        t = self.stack.enter_context(self.nc.psum_tensor(name, list(shape), dtype))
        b = Buf(name, t)
        b.excl = True
        return b

    def dram(self, name, shape, dtype, kind):
        t = self.nc.dram_tensor(name, list(shape), dtype, kind=kind)
        return Buf(name, t.ap())

    def _wait(self, en, ev):
        if ev[0] == "eng":
            _, x, n = ev
            if x == en and (en == "pe" or not SAME_SYNC):
                return
            if self.known[en].get(x, 0) >= n:
                return
            self.known[en][x] = n
            s = self.esem[x]
            self.prog[en].append(lambda e, s=s, n=n: e.wait_ge(s, n))
        else:
            _, s, n = ev
            key = ("d", id(s))
            if self.known[en].get(key, 0) >= n:
                return
            self.known[en][key] = n
            self.prog[en].append(lambda e, s=s, n=n: e.wait_ge(s, n))

    def _deps(self, en, reads, writes):
        evs = []
        for r in reads:
            if r.last_write is not None:
                evs.append(r.last_write)
            if r.excl:
                evs.extend(v for kk, v in r.readers.items() if kk != en)
        for w in writes:
            if w.last_write is not None:
                evs.append(w.last_write)
            evs.extend(w.readers.values())
        for ev in evs:
            self._wait(en, ev)

    def op(self, en, meth, reads, writes, *args, **kwargs):
        fn = lambda e: getattr(e, meth)(*args, **kwargs)
        self._deps(en, reads, writes)
        self.count[en] += 1
        idx = self.count[en]
        s = self.esem[en]
        self.prog[en].append(lambda e, fn=fn, s=s: fn(e).then_inc(s, 1))
        ev = ("eng", en, idx)
        for r in reads:
            r.readers[en] = ev
        for w in writes:
            w.last_write = ev
            w.readers = {}
        return ev

    def dma(self, en, out, in_, reads=(), writes=(), owner=None, is_output=False, **kw):
        self._deps(en, reads, writes)
        if owner is None:
            owner = writes[0] if writes else reads[0]
        if en not in owner.dsem:
            if self.free_sems.get(en):
                owner.dsem[en], owner.dcount[en] = self.free_sems[en].pop()
            else:
                owner.dsem[en] = self.new_sem("d")
                owner.dcount[en] = 0
        owner.dcount[en] += 16
        s, n = owner.dsem[en], owner.dcount[en]
        self.prog[en].append(lambda e, s=s: e.dma_start(out=out, in_=in_, **kw).then_inc(s, 16))
        ev = ("dma", s, n)
        self.all_dma[id(s)] = ev
        for r in reads:
            r.readers[("d", id(s))] = ev
        for w in writes:
            w.last_write = ev
            w.readers = {}
        if is_output:
            self.out_events.append(ev)
        return ev

    def collective(self, in_buf, out_buf, n_ranks, i_ap=None, o_ap=None):
        en = "pool"
        self._deps(en, [in_buf], [out_buf])
        s = self.new_sem("cc")
        i_ap = in_buf.t if i_ap is None else i_ap
        o_ap = out_buf.t if o_ap is None else o_ap
        self.prog[en].append(lambda e, s=s: e.collective_compute(
            "AllGather", mybir.AluOpType.bypass, replica_groups=[list(range(n_ranks))],
            ins=[i_ap.opt()], outs=[o_ap.opt()]).then_inc(s, 1))
        ev = ("dma", s, 1)
        self.all_dma[id(s)] = ev
        in_buf.readers[("d", id(s))] = ev
        out_buf.last_write = ev
        out_buf.readers = {}
        return ev

    def finish(self, en="sp"):
        for ev in self.out_events:
            self._wait(en, ev)
        nc = self.nc
        with nc.Block() as block:
            m = {"pe": block.tensor, "act": block.scalar, "dve": block.vector,
                 "pool": block.gpsimd, "sp": block.sync}
            for en in self.ENG:
                plist = self.prog[en]
                def body(e, plist=plist):
                    for f in plist:
                        f(e)
                m[en](body)
        self.stack.close()


def _k_barrier(self):
    evs = [("eng", x, self.count[x]) for x in self.ENG if self.count[x] > 0]
    evs += list(self.all_dma.values())
    for en in self.ENG:
        for ev in evs:
            if ev[0] == "eng" and ev[1] == en:
                n = ev[2]
                if self.known[en].get(en, 0) < n:
                    self.known[en][en] = n
                    s = self.esem[en]
                    self.prog[en].append(lambda e, s=s, n=n: e.wait_ge(s, n))
            else:
                self._wait(en, ev)


K.barrier = _k_barrier

import math

D = 1024
HID = 4096
TT = 512
ALPHA = 8.0 ** 0.25
LN_EPS = 1e-5
NEG = -30000.0
C1 = 6.28125
C2 = 2 * math.pi - C1


class Ctx:
    pass


def setup_common(k, T, consts):
    c = Ctx()
    c.k = k
    c.T = T
    c.NT = T // TT
    c.ones_f = k.sbuf("ones_f", [128, 128], F32)
    k.op("dve", "memset", [], [c.ones_f], c.ones_f[:], 1.0 / D)
    c.ident_b = k.sbuf("ident_b", [128, 128], BF16)
    k.dma("pool", c.ident_b[:], consts["ident"][:, :], writes=[c.ident_b])
    c.ps = [k.psum(f"ps{i}", [128, 512], F32) for i in range(8)]
    c.psi = 0
    c.pool = [0, 1, 2, 3]
    return c


def next_ps(c):
    p = c.ps[c.pool[c.psi % len(c.pool)]]
    c.psi += 1
    return p


def ln_tile(c, r, gcol, bcol, tmp, lnbuf):
    k = c.k
    ones_f = c.ones_f
    ps_mean_b, ps_msq_b = next_ps(c), next_ps(c)
    w_ = tmp["w"]

    class _V:
        def __init__(self, b):
            self.b = b

        def __getitem__(self, key):
            return self.b[:, 0:w_]
    mean_sb, rstd, nmr = (tmp[n] for n in ("mean_sb", "rstd", "nmr"))
    sqs = [tmp["sq0"], tmp["sq1"]]
    for d in range(8):
        sq = sqs[d % 2]
        k.op("act", "activation", [r.sub(d)], [sq], out=sq[:], in_=r[:, d, :], func=AF.Square)
        k.op("pe", "matmul", [r.sub(d), ones_f], [ps_mean_b], ps_mean_b[:, 0:w_], ones_f[:], r[:, d, :], start=(d == 0), stop=(d == 7))
        k.op("pe", "matmul", [sq, ones_f], [ps_msq_b], ps_msq_b[:, 0:w_], ones_f[:], sq[:], start=(d == 0), stop=(d == 7))
    k.op("act", "activation", [ps_mean_b], [mean_sb], out=mean_sb[:], in_=ps_mean_b[:, 0:w_], func=AF.Copy)
    k.op("act", "activation", [ps_mean_b], [rstd], out=rstd[:], in_=ps_mean_b[:, 0:w_], func=AF.Square)
    k.op("dve", "tensor_tensor", [ps_msq_b, rstd], [rstd], out=rstd[:], in0=ps_msq_b[:, 0:w_], in1=rstd[:], op=ALU.subtract)
    k.op("dve", "tensor_scalar_add", [rstd], [rstd], out=rstd[:], in0=rstd[:], scalar1=LN_EPS)
    k.op("act", "activation", [rstd], [rstd], out=rstd[:], in_=rstd[:], func=AF.Ln)
    k.op("act", "activation", [rstd], [rstd], out=rstd[:], in_=rstd[:], func=AF.Exp, scale=-0.5)
    k.op("dve", "scalar_tensor_tensor", [mean_sb, rstd], [nmr], out=nmr[:], in0=mean_sb[:], scalar=-1.0, in1=rstd[:],
         op0=ALU.mult, op1=ALU.mult)
    for d in range(8):
        k.op("dve", "tensor_tensor", [r.sub(d), rstd], [r.sub(d)], out=r[:, d, :], in0=r[:, d, :], in1=rstd[:], op=ALU.mult)
        k.op("pool", "tensor_tensor", [r.sub(d), nmr], [r.sub(d)], out=r[:, d, :], in0=r[:, d, :], in1=nmr[:], op=ALU.add)
        k.op("act", "activation", [r.sub(d), lnbuf], [r.sub(d)], out=r[:, d, :], in_=r[:, d, :], func=AF.Identity,
             scale=gcol[:, d:d + 1], bias=bcol[:, d:d + 1])


def ln_tmp(k, tt=TT):
    d = {n: k.sbuf(n, [128, tt], F32) for n in ("sq0", "sq1", "mean_sb", "rstd", "nmr")}
    d["w"] = tt
    return d


def mlp_phase(c, src, dst, wup, wdn, lnp, li, dst_is_output=False):
    k = c.k
    k.push()
    c.pool = list(range(8))
    wup_sb = k.sbuf("wup_sb", [128, 8, HID], BF16)
    wdn_sb = k.sbuf("wdn_sb", [128, 32, D], BF16)
    h_f = k.sbuf("h_f", [128, 8, TT], F32)
    h_bs = [k.sbuf("h_b0", [128, 8, TT], BF16)] * 2
    act = k.sbuf("act", [128, 32, TT], BF16)
    rl = [k.sbuf(f"rl{i}", [128, TT], BF16) for i in range(2)]
    tmp = ln_tmp(k)
    r = h_f
    wup_v = wup.t.rearrange("(k p) n -> p k n", p=128)
    wdn_v = wdn.t.rearrange("(k p) n -> p k n", p=128)
    for kk in range(8):
        k.dma("pool", wup_sb[:, kk, :], wup_v[:, kk, :], writes=[wup_sb], owner=wup_sb)
    for kk in range(32):
        k.dma("pool", wdn_sb[:, kk, :], wdn_v[:, kk, :], writes=[wdn_sb], owner=wdn_sb)
    src_v = src.t.rearrange("(k p) t -> p k t", p=128)
    dst_v = dst.t.rearrange("(k p) t -> p k t", p=128)
    gcol = lnp[:, li, 1, 0, :]
    bcol = lnp[:, li, 1, 1, :]
    for ti in range(c.NT):
        ts = slice(ti * TT, (ti + 1) * TT)
        h_b = h_bs[ti % 2]
        k.dma("sp", h_f[:], src_v[:, :, ts], reads=[src.sub(ti)], writes=[h_f.sub(d) for d in range(8)], owner=h_f)
        k.dma("pool", h_b[:], src_v[:, :, ts], reads=[src.sub(ti)], writes=[h_b], owner=h_b)
        for cc in range(32):
            p = next_ps(c)
            for kk in range(8):
                k.op("pe", "matmul", [wup_sb, h_b], [p], p[:], wup_sb[:, kk, cc * 128:(cc + 1) * 128], h_b[:, kk, :],
                     start=(kk == 0), stop=(kk == 7))
            rb = rl[cc % 2]
            k.op("act", "activation", [p], [rb], out=rb[:], in_=p[:], func=AF.Relu)
            k.op("pool", "tensor_tensor", [rb], [act.sub(cc)], out=act[:, cc, :], in0=rb[:], in1=rb[:], op=ALU.mult)
        for d in range(8):
            p = next_ps(c)
            for cc in range(32):
                k.op("pe", "matmul", [wdn_sb, act.sub(cc)], [p], p[:], wdn_sb[:, cc, d * 128:(d + 1) * 128], act[:, cc, :],
                     start=(cc == 0), stop=(cc == 31))
            k.op("dve", "scalar_tensor_tensor", [h_f.sub(d), p], [r.sub(d)], out=r[:, d, :], in0=h_f[:, d, :], scalar=ALPHA,
                 in1=p[:], op0=ALU.mult, op1=ALU.add)
        ln_tile(c, r, gcol, bcol, tmp, lnp)
        k.dma("sp", dst_v[:, :, ts], r[:], reads=[r.sub(d) for d in range(8)], writes=[dst.sub(ti)], owner=r,
              is_output=dst_is_output)
    k.pop()
    c.pool = [0, 1, 2, 3]


class AttnRes:
    def __init__(self, c):
        k = c.k
        self.c = c
        self.n = 0
        self.s_ps = [(c.ps[4], 0), (c.ps[5], 0)]
        self.pT_ps = [(c.ps[6], 0)]
        self.o_ps = [(c.ps[7], 0)]
        self.sm = [k.sbuf(f"sm{i}", [128, 256], F32) for i in range(2)]
        self.p = [k.sbuf(f"p{i}", [128, 256], BF16) for i in range(3)]
        self.dg = [k.sbuf(f"dg{i}", [128, 128], BF16) for i in range(3)]
        self.pT = [k.sbuf(f"pT{i}", [128, 256], BF16) for i in range(3)]
        self.st = [k.sbuf(f"st{i}", [128, 8], F32) for i in range(4)]


def attn_s0(A, it):
    k = A.c.k
    i = A.n
    A.n += 1
    it["i"] = i
    pbuf, off = A.s_ps[i % 2]
    sp = pbuf.sub(("s", off))
    it["s_ps"] = (sp, pbuf[:, off:off + 256])
    qb, qap = it["q"]
    kb, kap = it["kT"]
    k.op("pe", "matmul", [qb, kb], [sp], it["s_ps"][1], qap, kap, start=True, stop=True)


def attn_s1(A, it):
    k = A.c.k
    i = it["i"]
    sp, sap = it["s_ps"]
    st = A.st[i % 4]
    p = A.p[i % 3]
    dg = A.dg[i % 3]
    scale = it["scale"]
    if it["mask"] is not None:
        sm = A.sm[i % 2]
        mb = it["mask"]
        k.op("dve", "scalar_tensor_tensor", [sp, mb], [sm], out=sm[:], in0=sap, scalar=scale, in1=mb[:],
             op0=ALU.mult, op1=ALU.add)
        src_b, src_ap, esc = sm, sm[:], 1.0
    else:
        src_b, src_ap, esc = sp, sap, scale
    k.op("dve", "memset", [], [st], st[:, 2:3], 0.0)
    k.op("dve", "reduce_max", [src_b], [st], out=st[:, 0:1], in_=src_ap, axis=AX.X)
    if it["sink"] is not None:
        sb, sk = it["sink"]
        k.op("dve", "tensor_tensor", [st, sb], [st], out=st[:, 0:1], in0=st[:, 0:1], in1=sk, op=ALU.max)
    k.op("dve", "tensor_scalar", [st], [st], out=st[:, 1:2], in0=st[:, 0:1], scalar1=-esc, scalar2=None, op0=ALU.mult)
    k.op("act", "activation", [src_b, st], [p, st], out=p[:], in_=src_ap, func=AF.Exp, bias=st[:, 1:2], scale=esc,
         accum_out=st[:, 2:3])
    if it["sink"] is not None:
        k.op("act", "activation", [sb, st], [st], out=st[:, 3:4], in_=sk, func=AF.Exp, bias=st[:, 1:2], scale=1.0)
        k.op("dve", "tensor_tensor", [st], [st], out=st[:, 2:3], in0=st[:, 2:3], in1=st[:, 3:4], op=ALU.add)
    k.op("dve", "reciprocal", [st], [st], out=st[:, 4:5], in_=st[:, 2:3])
    k.op("dve", "tensor_scalar", [A.c.ident_b, st], [dg], out=dg[:], in0=A.c.ident_b[:], scalar1=st[:, 4:5], scalar2=None,
         op0=ALU.mult)


def attn_s2(A, it):
    k = A.c.k
    i = it["i"]
    p = A.p[i % 3]
    dg = A.dg[i % 3]
    pbuf, off = A.pT_ps[0]
    tp = pbuf.sub(("t", off))
    for kb in range(2):
        k.op("pe", "matmul", [p, dg], [tp], pbuf[:, off + kb * 128: off + (kb + 1) * 128], p[:, kb * 128:(kb + 1) * 128], dg[:],
             start=True, stop=True)
    pT = A.pT[i % 3]
    k.op("act", "activation", [tp], [pT], out=pT[:], in_=pbuf[:, off:off + 256], func=AF.Copy)


def attn_s3(A, it):
    k = A.c.k
    i = it["i"]
    pT = A.pT[i % 3]
    pbuf, off = A.o_ps[0]
    op_ = pbuf.sub(("o", off))
    vb, vaps = it["v"]
    for kb in range(2):
        k.op("pe", "matmul", [pT, vb], [op_], pbuf[:, off:off + 128], vaps[kb], pT[:, kb * 128:(kb + 1) * 128],
             start=(kb == 0), stop=(kb == 1))
    ob, oap = it["out"]
    pb = it["pb"]
    k.op("dve", "tensor_copy", [op_], [ob], out=oap, in_=pbuf[pb:pb + 64, off:off + 128])


def attn_run(A, items):
    n = len(items)
    for i in range(n + 2):
        if i < n:
            attn_s0(A, items[i])
            attn_s1(A, items[i])
        if 0 <= i - 1 < n:
            attn_s2(A, items[i - 1])
        if 0 <= i - 2 < n:
            attn_s3(A, items[i - 2])


def rope_tables(c, pos_i, invf, cos_t, sin_t, tmp):
    k = c.k
    ang, kf, ki, gg = tmp["ang"], tmp["kf"], tmp["ki"], tmp["gg"]
    k.op("dve", "tensor_copy", [pos_i], [ang], out=ang[:], in_=pos_i[:])
    k.op("dve", "tensor_scalar", [ang, invf], [ang], out=ang[:], in0=ang[:], scalar1=invf[:, 0:1], scalar2=None, op0=ALU.mult)
    for dst, shift in ((sin_t, 0.0), (cos_t, math.pi / 2)):
        if shift:
            k.op("dve", "tensor_scalar_add", [ang], [dst], out=dst[:], in0=ang[:], scalar1=shift)
            src = dst
        else:
            src = ang
        k.op("dve", "tensor_scalar", [src], [kf], out=kf[:], in0=src[:], scalar1=1.0 / (2 * math.pi), scalar2=None, op0=ALU.mult)
        k.op("dve", "tensor_copy", [kf], [ki], out=ki[:], in_=kf[:])
        k.op("dve", "tensor_copy", [ki], [kf], out=kf[:], in_=ki[:])
        k.op("dve", "scalar_tensor_tensor", [kf, src], [dst], out=dst[:], in0=kf[:], scalar=-C1, in1=src[:], op0=ALU.mult, op1=ALU.add)
        k.op("dve", "scalar_tensor_tensor", [kf, dst], [dst], out=dst[:], in0=kf[:], scalar=-C2, in1=dst[:], op0=ALU.mult, op1=ALU.add)
        k.op("dve", "tensor_scalar", [dst], [gg], out=gg[:], in0=dst[:], scalar1=math.pi, scalar2=None, op0=ALU.is_gt)
        k.op("dve", "scalar_tensor_tensor", [gg, dst], [dst], out=dst[:], in0=gg[:], scalar=-2 * math.pi, in1=dst[:], op0=ALU.mult, op1=ALU.add)
        k.op("act", "activation", [dst], [dst], out=dst[:], in_=dst[:], func=AF.Sin)


class PV:
    pass


def proj_chunk(c, w_sb, wbuf, col0, h_b, w=TT):
    k = c.k
    p = next_ps(c)
    for kk in range(8):
        k.op("pe", "matmul", [wbuf, h_b], [p], p[:, 0:w], w_sb[:, kk, col0:col0 + 128], h_b[:, kk, :], start=(kk == 0), stop=(kk == 7))
    return p


def rope_chunk(c, p, rot_b, cos_t, sin_t, rs, dst_buf, dst_ap, j):
    k = c.k
    qa, t1, t2 = rs["qa"][j % 2], rs["t1"][j % 2], rs["t2"][j % 2]
    k.op("act", "activation", [p], [qa], out=qa[:], in_=p[:], func=AF.Copy)
    p2 = next_ps(c)
    k.op("pe", "matmul", [rot_b, qa], [p2], p2[:], rot_b[:], qa[:], start=True, stop=True)
    k.op("dve", "tensor_tensor", [p, cos_t], [t1], out=t1[:], in0=p[:], in1=cos_t[:], op=ALU.mult)
    k.op("dve", "tensor_tensor", [p2, sin_t], [t2], out=t2[:], in0=p2[:], in1=sin_t[:], op=ALU.mult)
    k.op("pool", "tensor_tensor", [t1, t2], [dst_buf], out=dst_ap, in0=t1[:], in1=t2[:], op=ALU.add)


def load_w(k, dst, src_dram, nk=8, eng="pool"):
    v = src_dram.t.rearrange("(k p) n -> p k n", p=128)
    for kk in range(nk):
        k.dma(eng, dst[:, kk, :], v[:, kk, :], writes=[dst], owner=dst)


def mem_kv(c, memT_b, wkv_sb, memKT, memV):
    k = c.k
    for cc in range(2):
        p = next_ps(c)
        for kk in range(8):
            k.op("pe", "matmul", [wkv_sb, memT_b], [p], p[:, 0:256], wkv_sb[:, kk, cc * 128:(cc + 1) * 128], memT_b[:, kk, :],
                 start=(kk == 0), stop=(kk == 7))
        k.op("act", "activation", [p], [memKT], out=memKT[:, cc, :], in_=p[:, 0:256], func=AF.Copy)
    for kb in range(2):
        p = next_ps(c)
        for kk in range(8):
            k.op("pe", "matmul", [wkv_sb, memT_b], [p], p[:, 0:256], memT_b[:, kk, kb * 128:(kb + 1) * 128], wkv_sb[:, kk, 256:512],
                 start=(kk == 0), stop=(kk == 7))
        k.op("act", "activation", [p], [memV], out=memV[:, kb, :], in_=p[:, 0:256], func=AF.Copy)


def mixer_b_phase(c, src, dst, G, L, li):
    k = c.k
    j = li - 2
    k.push()
    w_in = k.sbuf("w_in", [128, 8, 1024], BF16)
    w_o = k.sbuf("w_o", [128, 8, 1024], BF16)
    wkv = k.sbuf("wkv", [128, 8, 512], BF16)
    load_w(k, wkv, L["mem_w_kv"])
    load_w(k, w_in, L["w_in"])
    load_w(k, w_o, L["w_o"])
    memKT = k.sbuf("memKT", [128, 2, 256], BF16)
    memV = k.sbuf("memV", [128, 2, 256], BF16)
    memT_b = k.sbuf("memT_b", [128, 8, 256], BF16)
    k.dma("pool", memT_b[:], G["memT"].t.rearrange("(k p) m -> p k m", p=128), writes=[memT_b])
    mem_kv(c, memT_b, wkv, memKT, memV)
    NB = c.T // 128
    kT = k.sbuf("kT_sb", [128, 2, 128 + c.T], BF16)
    v = k.sbuf("v_sb", [128, NB + 1, 2, 128], BF16)
    if "kvt_all" in G:
        k.dma("pool", kT[:, :, 128:], G["kshT"][:, :, :], reads=[G["kshT"]], writes=[kT], owner=kT)
        k.dma("pool", v[:, 1:, :, :], G["vsh"].t.rearrange("(b p) (k d) -> p b k d", p=128, k=2), reads=[G["vsh"]], writes=[v], owner=v)
        k.push()
        kvt = k.sbuf("kvt", [128, 8, 512], F32)
        kacc = k.sbuf("kacc", [128, 512], F32)
        gates = G["gates"]
        k.dma("sp", kvt[:], G["kvt_all"].t.rearrange("(r p) x -> p r x", p=128), reads=[G["kvt_all"]], writes=[kvt])
        k.op("dve", "tensor_scalar", [kvt, gates], [kacc], out=kacc[:], in0=kvt[:, 0, :], scalar1=gates[:, 8:9], scalar2=None, op0=ALU.mult)
        for r_ in range(1, 8):
            k.op("dve", "scalar_tensor_tensor", [kvt, gates, kacc], [kacc], out=kacc[:], in0=kvt[:, r_, :], scalar=gates[:, 8 + r_:9 + r_],
                 in1=kacc[:], op0=ALU.mult, op1=ALU.add)
        k.op("dve", "tensor_copy", [kacc], [kT], out=kT[:, :, 0:128], in_=kacc[:, 0:256].rearrange("p (k t) -> p k t", k=2))
        k.op("dve", "tensor_copy", [kacc], [v], out=v[:, 0, :, :], in_=kacc[:, 256:512].rearrange("p (k d) -> p k d", k=2))
        k.pop()
    else:
        k.dma("pool", kT[:], G["kTd"][:, :, :], writes=[kT])
        k.dma("pool", v[:], G["vd"].t.rearrange("(b p) k d -> p b k d", p=128), writes=[v])
    h_f = k.sbuf("h_f", [128, 8, TT], F32)
    h_bs = [k.sbuf(f"h_b{i}", [128, 8, TT], BF16) for i in range(2)]
    pos_i = k.sbuf("pos_i", [128, TT], I32)
    cos_t = k.sbuf("cos_t", [128, TT], F32)
    sin_t = k.sbuf("sin_t", [128, TT], F32)
    rt = {"ang": k.sbuf("ang", [128, TT], F32), "kf": k.sbuf("kf", [128, TT], F32), "ki": k.sbuf("ki", [128, TT], I32),
          "gg": k.sbuf("gg", [128, TT], F32)}
    rs = {"qa": [k.sbuf(f"qa{i}", [128, TT], BF16) for i in range(2)],
          "t1": [rt["ang"], rt["kf"]], "t2": [rt["gg"], k.sbuf("t2b", [128, TT], F32)]}
    qT = k.sbuf("qT", [128, 8, TT], BF16)
    oT = k.sbuf("oT", [128, 8, TT], BF16)
    tmp = ln_tmp(k)
    A = AttnRes(c)
    src_v = src.t.rearrange("(k p) t -> p k t", p=128)
    dst_v = dst.t.rearrange("(k p) t -> p k t", p=128)
    lnp = G["lnp"]
    gcol = lnp[:, li, 0, 0, :]
    bcol = lnp[:, li, 0, 1, :]
    scale = 0.125
    for ti in range(c.NT):
        ts = slice(ti * TT, (ti + 1) * TT)
        h_b = h_bs[ti % 2]
        k.dma("sp", h_f[:], src_v[:, :, ts], reads=[src.sub(ti)], writes=[h_f.sub(d) for d in range(8)], owner=h_f)
        k.dma("pool", h_b[:], src_v[:, :, ts], reads=[src.sub(ti)], writes=[h_b], owner=h_b)
        k.dma("sp", pos_i[:], G["pos"][:, ts], writes=[pos_i])
        rope_tables(c, pos_i, G["invf"], cos_t, sin_t, rt)
        for cc in range(8):
            p = proj_chunk(c, w_in, w_in, cc * 128, h_b)
            if cc < 6:
                rope_chunk(c, p, G["rot_b"], cos_t, sin_t, rs, qT.sub(cc), qT[:, cc, :], cc)
            else:
                k.op("act", "activation", [p], [qT.sub(cc)], out=qT[:, cc, :], in_=p[:], func=AF.Copy)
        items = []
        for blk in range(4):
            gb = ti * 4 + blk
            qs = slice(blk * 128, (blk + 1) * 128)
            for hd in range(12):
                cc, pb = hd // 2, (hd % 2) * 64
                kv = hd // 6
                mask = G["mask_first"] if gb == 0 else G["mask_band"]
                items.append(dict(q=(qT.sub(cc), qT[pb:pb + 64, cc, qs]),
                                  kT=(kT, kT[pb:pb + 64, kv, gb * 128: gb * 128 + 256]),
                                  v=(v, [v[:, gb + kb, kv, :] for kb in range(2)]),
                                  mask=mask, sink=(G["sinks"], G["sinks"][:, j * 12 + hd: j * 12 + hd + 1]),
                                  scale=scale, out=(oT.sub((cc, blk, pb)), oT[pb:pb + 64, cc, qs]), pb=pb))
            for hd in range(4):
                cc, pb = 6 + hd // 2, (hd % 2) * 64
                mc = hd // 2
                items.append(dict(q=(qT.sub(cc), qT[pb:pb + 64, cc, qs]),
                                  kT=(memKT, memKT[pb:pb + 64, mc, :]),
                                  v=(memV, [memV[:, kb, mc * 128:(mc + 1) * 128] for kb in range(2)]),
                                  mask=None, sink=None, scale=scale,
                                  out=(oT.sub((cc, blk, pb)), oT[pb:pb + 64, cc, qs]), pb=pb))
        attn_run(A, items)
        oT_reads = [oT.sub((cc, blk, pb)) for cc in range(8) for blk in range(4) for pb in (0, 64)]
        r = h_f
        for d in range(8):
            p = next_ps(c)
            for cc in range(8):
                rd = [w_o] + (oT_reads if (cc == 0 and d == 0) or (cc == 7 and d == 7) else [])
                k.op("pe", "matmul", rd, [p], p[:], w_o[:, cc, d * 128:(d + 1) * 128], oT[:, cc, :], start=(cc == 0), stop=(cc == 7))
            k.op("dve", "scalar_tensor_tensor", [h_f.sub(d), p], [r.sub(d)], out=r[:, d, :], in0=h_f[:, d, :], scalar=ALPHA,
                 in1=p[:], op0=ALU.mult, op1=ALU.add)
        ln_tile(c, r, gcol, bcol, tmp, lnp)
        k.dma("sp", dst_v[:, :, ts], r[:], reads=[r.sub(d) for d in range(8)], writes=[dst.sub(ti)], owner=r)
    k.pop()


def make_consts():
    ident = np.eye(128, dtype=np.float32)
    rot = np.zeros((128, 128), np.float32)
    for m in range(128):
        hb, i = (m // 64) * 64, m % 64
        if i < 32:
            rot[hb + i + 32, m] = -1.0
        else:
            rot[hb + i - 32, m] = 1.0
    qi = np.arange(128)[:, None]
    kj = np.arange(256)[None, :]
    diff = kj - qi
    band = (diff >= 1) & (diff <= 128)
    mask_band = np.where(band, 0.0, NEG).astype(np.float32)
    mask_first = np.where(band & (kj >= 128), 0.0, NEG).astype(np.float32)
    invf = (10000.0 ** (-(np.arange(0, 64, 2, dtype=np.float32)) / 64)).astype(np.float32)
    invf128 = invf[np.arange(128) % 32][:, None].astype(np.float32)
    return dict(ident=ident, rot=rot, mask_band=mask_band, mask_first=mask_first, invf=invf128)


def build_b(T, layers=(2, 3)):
    nc = bass.Bass("TRN2", target_bir_lowering=False)
    k = K(nc)
    NB = T // 128
    inp = lambda n, s, dt=F32: k.dram(n, s, dt, "ExternalInput")
    hT = inp("hT", [D, T])
    pos = inp("pos", [128, T], I32)
    kTd = inp("kT", [128, 2, 128 + T])
    vd = inp("v", [128 + T, 2, 128])
    memT = inp("memT", [D, 256])
    cd = {n: inp("c_" + n, list(a.shape)) for n, a in make_consts().items() if n != "mask_first"}
    cd["mask_first"] = inp("c_mask_first", [128, 256])
    sinks = inp("sinks", [128, 24])
    lnp_d = inp("lnp", [128, 4 * 2 * 2 * 8])
    Ls = {}
    for li in layers:
        Ls[li] = dict(w_in=inp(f"w_in{li}", [D, 1024]), w_o=inp(f"w_o{li}", [D, D]), mem_w_kv=inp(f"mwkv{li}", [D, 512]),
                      w_up=inp(f"w_up{li}", [D, HID]), w_dn=inp(f"w_dn{li}", [HID, D]))
    outT = k.dram("outT", [D, T], F32, "ExternalOutput")
    hmid = k.dram("hmid", [D, T], F32, "Internal")
    hnext = k.dram("hnext", [D, T], F32, "Internal")

    c = setup_common(k, T, cd)
    G = {}
    G["pos"] = pos
    G["kTd"], G["vd"], G["memT"] = kTd, vd, memT
    for n in ("mask_band", "mask_first"):
        G[n] = k.sbuf(n, [128, 256], F32)
        k.dma("sp", G[n][:], cd[n][:, :], writes=[G[n]])
    G["rot_b"] = k.sbuf("rot_b", [128, 128], BF16)
    k.dma("pool", G["rot_b"][:], cd["rot"][:, :], writes=[G["rot_b"]])
    G["invf"] = k.sbuf("invf", [128, 1], F32)
    k.dma("sp", G["invf"][:], cd["invf"][:, :], writes=[G["invf"]])
    G["sinks"] = k.sbuf("sinks_sb", [128, 24], F32)
    k.dma("sp", G["sinks"][:], sinks[:, :], writes=[G["sinks"]])
    lnp = k.sbuf("lnp_sb", [128, 4, 2, 2, 8], F32)
    k.dma("sp", lnp[:], lnp_d.t.rearrange("p (a b c d) -> p a b c d", a=4, b=2, c=2), writes=[lnp])
    G["lnp"] = lnp
    src = hT
    for n_, li in enumerate(layers):
        mixer_b_phase(c, src, hmid, G, Ls[li], li)
        last = (n_ == len(layers) - 1)
        dst = outT if last else hnext
        mlp_phase(c, hmid, dst, Ls[li]["w_up"], Ls[li]["w_dn"], lnp, li, dst_is_output=last)
        src = dst
    k.finish()
    return nc


DK = 128
NH = 6
BIG = 30000.0
NORM_EPS = 1e-6
C_Q, C_K, C_V, C_Z, C_QM, C_AB = 0, 768, 1536, 2304, 3072, 3328
A_COLS = 3344
TA = 256
NBK = TA // 128


def make_consts_a():
    p = np.arange(128)
    same = (p[:, None] // 64) == (p[None, :] // 64)
    U = (same & (p[:, None] <= p[None, :])).astype(np.float32)
    CH = same.astype(np.float32)
    ML = np.where(same & (p[:, None] > p[None, :]), 0.0, BIG).astype(np.float32)
    MA = np.where(same & (p[None, :] >= p[:, None]), 0.0, -BIG).astype(np.float32)
    ML3 = np.tile(ML[:, None, :], (1, 3, 1)).reshape(128, 384)
    MA3 = np.tile(MA[:, None, :], (1, 3, 1)).reshape(128, 384)
    ident3 = np.tile(np.eye(128, dtype=np.float32)[:, None, :], (1, 3, 1)).reshape(128, 384)
    return dict(U=U, CH=CH, ML3=ML3, MA3=MA3, ident3=ident3)


class ARes:
    pass


def mixer_a_phase(c, src, dst, G, L, li, mode):
    k = c.k
    full = (mode == "full")
    NV = 128 if full else 256
    T = c.T
    k.push()
    c.pool = [0, 1]
    PB = c.ps
    w_in = k.sbuf("w_in", [128, 8, A_COLS], BF16)
    load_w(k, w_in, L["w_in"])
    convw = k.sbuf("convw", [128, 18, 4], F32)
    k.dma("sp", convw[:], L["convw"][:, :, :], writes=[convw])
    dconv = k.sbuf("dconv", [128, 18, 4, 128], BF16)
    for cc in range(18):
        for j in range(4):
            k.op("dve", "tensor_scalar", [c.ident_b, convw], [dconv], out=dconv[:, cc, j, :], in0=c.ident_b[:],
                 scalar1=convw[:, cc, j:j + 1], scalar2=None, op0=ALU.mult)
    small = k.sbuf("small_a", [128, 32], F32)
    k.dma("sp", small[:, 0:18], L["gate_c"][:, :], writes=[small])
    k.op("act", "activation", [small], [small], out=small[:, 12:18], in_=small[:, 12:18], func=AF.Exp)
    k.op("dve", "tensor_scalar", [small], [small], out=small[:, 12:18], in0=small[:, 12:18], scalar1=-1.0, scalar2=None, op0=ALU.mult)
    k.op("dve", "memset", [], [small], small[:, 18:19], NORM_EPS)
    k.op("dve", "memset", [], [small], small[:, 19:20], 1.0)
    k.op("dve", "memset", [], [small], small[:, 20:21], -0.5 * math.log(128.0))
    k.op("dve", "memset", [], [small], small[:, 21:22], 0.0)
    ones_b = k.sbuf("ones_b", [128, 128], BF16)
    k.op("dve", "memset", [], [ones_b], ones_b[:], 1.0)
    ones1 = k.sbuf("ones1", [128, 128], F32)
    k.op("dve", "memset", [], [ones1], ones1[:], 1.0)
    cf = {}
    for n, w_ in (("U", 128), ("CH", 128), ("ML3", 384), ("MA3", 384), ("ident3", 384)):
        cf[n] = k.sbuf("c_" + n, [128, w_], F32)
        k.dma("sp", cf[n][:], G["ca_" + n][:, :], writes=[cf[n]])
    ident_f = k.sbuf("ident_f", [128, 128], F32)
    k.dma("sp", ident_f[:], G["c_ident"][:, :], writes=[ident_f])
    S_f = k.sbuf("S_f", [128, NH, NV], F32)
    S_b = k.sbuf("S_b", [128, NH, NV], BF16)
    if full:
        k.op("dve", "memset", [], [S_f], S_f[:], 0.0)
        k.push()
        pq_sb = k.sbuf("pq_sb", [128, NH, 256], F32)
        PT_sb = k.sbuf("PT_sb", [128, 128], F32)
        tf = k.sbuf("tf", [128, 128], F32)
        gates = G["gates"]
        for r_ in L["fold_ranks"]:
            k.dma("sp", pq_sb[:], L["pq_all"](r_), reads=[L["pq_all_buf"]], writes=[pq_sb], owner=pq_sb)
            for h in range(NH):
                tpp = next_ps(c)
                k.op("pe", "transpose", [pq_sb, ident_f], [tpp], tpp[:, 0:128], pq_sb[:, h, 128:256], ident_f[:])
                k.op("act", "activation", [tpp], [PT_sb], out=PT_sb[:], in_=tpp[:, 0:128], func=AF.Copy)
                np_ = next_ps(c)
                k.op("pe", "matmul", [PT_sb, S_f], [np_], np_[:, 0:128], PT_sb[:], S_f[:, h, :], start=True, stop=True)
                k.op("dve", "tensor_tensor", [np_, pq_sb], [tf], out=tf[:], in0=np_[:, 0:128], in1=pq_sb[:, h, 0:128], op=ALU.add)
                k.op("dve", "tensor_tensor", [tf, S_f], [tf], out=tf[:], in0=tf[:], in1=S_f[:, h, :], op=ALU.subtract)
                k.op("dve", "scalar_tensor_tensor", [tf, S_f, gates], [S_f], out=S_f[:, h, :], in0=tf[:], scalar=gates[:, L["gate_col"](r_)],
                     in1=S_f[:, h, :], op0=ALU.mult, op1=ALU.add)
        k.pop()
    else:
        k.op("dve", "memset", [], [S_f], S_f[:], 0.0)
        for h in range(NH):
            k.op("dve", "tensor_copy", [ident_f, S_f], [S_f], out=S_f[:, h, 128:256], in_=ident_f[:])
    k.op("act", "activation", [S_f], [S_b], out=S_b[:], in_=S_f[:], func=AF.Copy)
    Sf = [S_f.sub(h) for h in range(NH)]
    Sb = [S_b.sub(h) for h in range(NH)]
    for h in range(NH):
        Sf[h].last_write = S_f.last_write
        Sb[h].last_write = S_b.last_write
    h_b = k.sbuf("h_b", [128, 8, TA], BF16)
    hh_b = k.sbuf("hh_b", [128, 8, 3], BF16)
    halo = k.sbuf("halo", [128, 18, 3], BF16)
    xcs = [k.sbuf(f"xc{i}", [128, TA + 3], BF16) for i in range(3)]
    ystore = k.sbuf("ystore", [128, 12, TA], BF16)
    tmp = ln_tmp(k, TA)
    ysil = [tmp["sq0"], tmp["sq1"]]
    rn = [tmp["mean_sb"], tmp["rstd"]]
    sqb = [k.sbuf(f"sqb{i}", [128, TA], BF16) for i in range(2)]
    qkv = k.sbuf("qkv", [128, 18, TA], BF16)
    gt = k.sbuf("gt", [128, NBK, 32], F32)
    src_v = src.t.rearrange("(k p) t -> p k t", p=128)
    if full:
        h_f = k.sbuf("h_f", [128, 8, TA], F32)
        zs = k.sbuf("zs", [128, 6, TA], BF16)
        qmT = k.sbuf("qmT", [128, 2, TA], BF16)
        oT = k.sbuf("oT", [128, 8, TA], BF16)
        orawT = k.sbuf("orawT", [128, 6, TA], F32)
        w_o = k.sbuf("w_o", [128, 8, 1024], BF16)
        load_w(k, w_o, L["w_o"])
        memKT = k.sbuf("memKT", [128, 2, 256], BF16)
        memV = k.sbuf("memV", [128, 2, 256], BF16)
        k.push()
        wkv = k.sbuf("wkv", [128, 8, 512], BF16)
        load_w(k, wkv, L["mem_w_kv"])
        memT_b = k.sbuf("memT_b", [128, 8, 256], BF16)
        k.dma("pool", memT_b[:], G["memT"].t.rearrange("(k p) m -> p k m", p=128), writes=[memT_b])
        mem_kv(c, memT_b, wkv, memKT, memV)
        k.pop()
        normw = k.sbuf("normw", [128, 1], F32)
        k.dma("sp", normw[:], L["normw"][:, :], writes=[normw])
        A = AttnRes(c)
        dst_v = dst.t.rearrange("(k p) t -> p k t", p=128)
        lnp = G["lnp"]
    def B6(n, dt=BF16):
        return k.sbuf(n, [128, NH, 128], dt)
    sv = k.sbuf("sv", [128, 64], F32)
    Eb, ET, eGr = B6("Eb", F32), B6("ET", F32), B6("eGr", F32)
    Lc = [B6("Lc0"), B6("Lc1")]
    Nc = [B6("Nc0"), B6("Nc1")]
    Yf, Yb = B6("Yf", F32), B6("Yb")
    Dg = Yf
    ATb = B6("ATb")
    Xk, Xv, kd = B6("Xk"), B6("Xv"), B6("kd")
    u_f = B6("u_f", F32)
    wT = B6("wT")
    qdT = B6("qdT")
    vnew = k.sbuf("vnew", [128, NH, NV], BF16)

    if "hl_all" in L:
        k.push()
        hl = k.sbuf("hl", [128, 8, 8, 3], F32)
        hacc = k.sbuf("hacc", [128, 8, 3], F32)
        k.dma("sp", hl[:], L["hl_all"].t.rearrange("(r k p) t -> p r k t", r=8, p=128), reads=[L["hl_all"]], writes=[hl])
        gates = G["gates"]
        k.op("dve", "tensor_scalar", [hl, gates], [hacc], out=hacc[:], in0=hl[:, 0, :, :], scalar1=gates[:, 8:9], scalar2=None, op0=ALU.mult)
        for r_ in range(1, 8):
            k.op("dve", "scalar_tensor_tensor", [hl, gates, hacc], [hacc], out=hacc[:], in0=hl[:, r_, :, :], scalar=gates[:, 8 + r_:9 + r_],
                 in1=hacc[:], op0=ALU.mult, op1=ALU.add)
        k.op("dve", "tensor_copy", [hacc], [hh_b], out=hh_b[:], in_=hacc[:])
        k.pop()
    else:
        k.dma("pool", hh_b[:], L["hhalo"].t.rearrange("(k p) t -> p k t", p=128), writes=[hh_b])
    for cc in range(18):
        p = next_ps(c)
        for kk in range(8):
            k.op("pe", "matmul", [w_in, hh_b], [p], p[:, 0:3], w_in[:, kk, cc * 128:(cc + 1) * 128], hh_b[:, kk, :],
                 start=(kk == 0), stop=(kk == 7))
        k.op("act", "activation", [p], [halo.sub(cc)], out=halo[:, cc, :], in_=p[:, 0:3], func=AF.Copy)

    for ti in range(T // TA):
        ts = slice(ti * TA, (ti + 1) * TA)
        k.dma("pool", h_b[:], src_v[:, :, ts], reads=[src.sub(ti)], writes=[h_b], owner=h_b)
        if full:
            k.dma("sp", h_f[:], src_v[:, :, ts], reads=[src.sub(ti)], writes=[h_f.sub(d) for d in range(8)], owner=h_f)
        c.pool = list(range(8))
        chunks = list(range(18)) if full else list(range(6, 18))

        def stage_a(cc):
            xc = xcs[cc % 3]
            p = proj_chunk(c, w_in, w_in, cc * 128, h_b, TA)
            k.op("pool", "tensor_copy", [halo.sub(cc)], [xc], out=xc[:, 0:3], in_=halo[:, cc, :])
            k.op("act", "activation", [p], [xc], out=xc[:, 3:TA + 3], in_=p[:, 0:TA], func=AF.Copy)
            k.op("pool", "tensor_copy", [xc], [halo.sub(cc)], out=halo[:, cc, :], in_=xc[:, TA:TA + 3])

        def stage_b(cc):
            xc = xcs[cc % 3]
            p2 = next_ps(c)
            for j in range(4):
                k.op("pe", "matmul", [dconv, xc], [p2], p2[:, 0:TA], dconv[:, cc, j, :], xc[:, j:j + TA], start=(j == 0), stop=(j == 3))
            if cc >= 12:
                k.op("act", "activation", [p2], [qkv.sub(cc)], out=qkv[:, cc, :], in_=p2[:, 0:TA], func=AF.Silu)
            else:
                k.op("act", "activation", [p2], [ystore.sub(cc)], out=ystore[:, cc, :], in_=p2[:, 0:TA], func=AF.Silu)

        for i_, cc in enumerate(chunks + [None]):
            if cc is not None:
                stage_a(cc)
            if i_ >= 1:
                stage_b(chunks[i_ - 1])
        if full:
            for hh in range(6):
                p = proj_chunk(c, w_in, w_in, C_Z + hh * 128, h_b, TA)
                k.op("act", "activation", [p], [zs.sub(hh)], out=zs[:, hh, :], in_=p[:, 0:TA], func=AF.Silu)
            for cc in range(2):
                p = proj_chunk(c, w_in, w_in, C_QM + cc * 128, h_b, TA)
                k.op("act", "activation", [p], [qmT.sub(cc)], out=qmT[:, cc, :], in_=p[:, 0:TA], func=AF.Copy)
        for cc in [x_ for x_ in chunks if x_ < 12]:
            sq, r_ = sqb[cc % 2], rn[cc % 2]
            k.op("act", "activation", [ystore.sub(cc)], [sq], out=sq[:], in_=ystore[:, cc, :], func=AF.Square)
            p3 = next_ps(c)
            k.op("pe", "matmul", [ones_b, sq], [p3], p3[:, 0:TA], ones_b[:], sq[:], start=True, stop=True)
            k.op("act", "activation", [p3, small], [r_], out=r_[:], in_=p3[:, 0:TA], func=AF.Ln, bias=small[:, 18:19])
            k.op("act", "activation", [r_, small], [r_], out=r_[:], in_=r_[:], func=AF.Exp, scale=-0.5,
                 bias=(small[:, 20:21] if cc < 6 else small[:, 21:22]))
            k.op("dve", "tensor_tensor", [ystore.sub(cc), r_], [qkv.sub(cc)], out=qkv[:, cc, :], in0=ystore[:, cc, :], in1=r_[:], op=ALU.mult)
        p = next_ps(c)
        for blk in range(NBK):
            for kk in range(8):
                k.op("pe", "matmul", [w_in, h_b], [p], p[:, blk * 12:(blk + 1) * 12], h_b[:, kk, blk * 128:(blk + 1) * 128],
                     w_in[:, kk, C_AB:C_AB + 12], start=(kk == 0), stop=(kk == 7))
        pv = p[:, 0:12 * NBK].rearrange("p (b x) -> p b x", b=NBK)
        k.op("dve", "tensor_tensor", [p, small], [gt], out=gt[:, :, 18:30], in0=pv, in1=small[:, 0:12].unsqueeze(1).to_broadcast([128, NBK, 12]),
             op=ALU.add)
        k.op("act", "activation", [gt], [gt], out=gt[:, :, 0:6], in_=gt[:, :, 18:24], func=AF.Exp)
        k.op("act", "activation", [gt, small], [gt], out=gt[:, :, 0:6], in_=gt[:, :, 0:6], func=AF.Ln, bias=small[:, 19:20])
        k.op("dve", "tensor_tensor", [gt, small], [gt], out=gt[:, :, 0:6], in0=gt[:, :, 0:6],
             in1=small[:, 12:18].unsqueeze(1).to_broadcast([128, NBK, 6]), op=ALU.mult)
        k.op("act", "activation", [gt], [gt], out=gt[:, :, 12:18], in_=gt[:, :, 24:30], func=AF.Exp, scale=-1.0)
        k.op("act", "activation", [gt, small], [gt], out=gt[:, :, 12:18], in_=gt[:, :, 12:18], func=AF.Ln, bias=small[:, 19:20])
        k.op("dve", "tensor_scalar", [gt], [gt], out=gt[:, :, 12:18], in0=gt[:, :, 12:18], scalar1=-1.0, scalar2=None, op0=ALU.mult)
        k.op("act", "activation", [gt], [gt], out=gt[:, :, 6:12], in_=gt[:, :, 12:18], func=AF.Exp)

        c.pool = [0, 1]
        for blk in range(NBK):
            bs = slice(blk * 128, (blk + 1) * 128)
            p = next_ps(c)
            k.op("pe", "matmul", [cf["U"], gt], [p], p[:, 0:6], cf["U"][:], gt[:, blk, 0:6], start=True, stop=True)
            k.op("pe", "matmul", [cf["CH"], gt], [p], p[:, 6:12], cf["CH"][:], gt[:, blk, 0:6], start=True, stop=True)
            k.op("dve", "tensor_copy", [p], [sv], out=sv[:, 0:6], in_=p[:, 0:6])
            k.op("dve", "tensor_copy", [p], [sv], out=sv[:, 30:36], in_=p[:, 6:12])
            k.op("dve", "tensor_scalar", [sv], [sv], out=sv[:, 6:12], in0=sv[:, 0:6], scalar1=-1.0, scalar2=None, op0=ALU.mult)
            k.op("dve", "tensor_tensor", [sv, gt], [sv], out=sv[:, 12:18], in0=sv[:, 0:6], in1=gt[:, blk, 12:18], op=ALU.add)
            k.op("act", "activation", [sv], [sv], out=sv[:, 18:24], in_=sv[:, 12:18], func=AF.Exp)
            k.op("dve", "tensor_tensor", [sv], [sv], out=sv[:, 24:30], in0=sv[:, 30:36], in1=sv[:, 0:6], op=ALU.subtract)
            k.op("act", "activation", [sv], [sv], out=sv[:, 24:30], in_=sv[:, 24:30], func=AF.Exp)
            k.op("dve", "tensor_tensor", [ident_f, sv], [Yf.sub(0), Yf.sub(1)], out=Dg[:], in0=ident_f[:].unsqueeze(1).to_broadcast([128, NH, 128]),
                 in1=sv[:, 0:6].unsqueeze(2).to_broadcast([128, NH, 128]), op=ALU.mult)
            for g in range(2):
                hs = slice(3 * g, 3 * g + 3)
                Dg3 = Dg[:, hs, :].rearrange("p h j -> p (h j)")
                for bank, mk in (((PB[2], "ML3"), (PB[3], "MA3"), (PB[4], None)) if full else ((PB[2], "ML3"), (PB[4], None))):
                    k.op("pe", "matmul", [ones1, Yf.sub(g)], [bank], bank[:, 0:384], ones1[:], Dg3, start=True, stop=(mk is None))
                    if mk is not None:
                        k.op("pe", "matmul", [ident_f, cf[mk]], [bank], bank[:, 0:384], ident_f[:], cf[mk][:], start=False, stop=True)
                for hh in range(3):
                    h = 3 * g + hh
                    k.op("act", "activation", [PB[2], sv], [Eb.sub(h)], out=Eb[:, h, :], in_=PB[2][:, hh * 128:(hh + 1) * 128], func=AF.Exp,
                         scale=-1.0, bias=sv[:, 12 + h:13 + h])
                    if full:
                        k.op("act", "activation", [PB[3], sv], [ET.sub(h)], out=ET[:, h, :], in_=PB[3][:, hh * 128:(hh + 1) * 128],
                             func=AF.Exp, scale=1.0, bias=sv[:, 6 + h:7 + h])
                k.op("act", "activation", [PB[4]], [eGr.sub(("g", g))], out=eGr[:, hs, :].rearrange("p h j -> p (h j)"),
                     in_=PB[4][:, 0:384], func=AF.Exp)
            EbR = [Eb.sub(h) for h in range(NH)]
            ETR = [ET.sub(h) for h in range(NH)]
            eGR = [eGr.sub(("g", g)) for g in range(2)]
            kq = qkv
            for g in range(2):
                hs = slice(3 * g, 3 * g + 3)
                for hh in range(3):
                    h = 3 * g + hh
                    k.op("pe", "matmul", [kq.sub(6 + h)], [PB[5]], PB[5][:, hh * 128:(hh + 1) * 128], kq[:, 6 + h, bs], kq[:, 6 + h, bs],
                         start=True, stop=True)
                for hh in range(3):
                    h = 3 * g + hh
                    if full:
                        k.op("pe", "matmul", [kq.sub(6 + h), kq.sub(h)], [PB[6]], PB[6][:, hh * 128:(hh + 1) * 128], kq[:, 6 + h, bs],
                             kq[:, h, bs], start=True, stop=True)
                k.op("dve", "tensor_tensor", [PB[5]] + EbR[3 * g:3 * g + 3], [Lc[0].sub(g)], out=Lc[0][:, hs, :].rearrange("p h j -> p (h j)"),
                     in0=PB[5][:, 0:384], in1=Eb[:, hs, :].rearrange("p h j -> p (h j)"), op=ALU.mult)
                if full:
                    k.op("dve", "tensor_tensor", [PB[6]] + ETR[3 * g:3 * g + 3], [ATb.sub(g)], out=ATb[:, hs, :].rearrange("p h j -> p (h j)"),
                         in0=PB[6][:, 0:384], in1=ET[:, hs, :].rearrange("p h j -> p (h j)"), op=ALU.mult)
            tp = PB[7]
            tpb = tp[:].bitcast(BF16)
            for h in range(NH):
                k.op("pe", "transpose", [Lc[0].sub(h // 3), c.ident_b], [tp], tpb[:, h * 128:(h + 1) * 128], Lc[0][:, h, :], c.ident_b[:])
            k.op("act", "activation", [tp], [Nc[0].sub(0), Nc[0].sub(1)], out=Nc[0][:].rearrange("p h j -> p (h j)"), in_=tpb[:, 0:768],
                 func=AF.Copy)
            for g in range(2):
                hs = slice(3 * g, 3 * g + 3)
                k.op("dve", "tensor_tensor", [cf["ident3"], Nc[0].sub(g)], [Yf.sub(g)], out=Yf[:, hs, :].rearrange("p h j -> p (h j)"),
                     in0=cf["ident3"][:], in1=Nc[0][:, hs, :].rearrange("p h j -> p (h j)"), op=ALU.subtract)
                k.op("act", "activation", [Yf.sub(g)], [Yb.sub(g)], out=Yb[:, hs, :], in_=Yf[:, hs, :], func=AF.Copy)
            cur = 0
            for lvl in range(1, 6):
                nxt = 1 - cur
                for g in range(2):
                    hs = slice(3 * g, 3 * g + 3)
                    for hh in range(3):
                        h = 3 * g + hh
                        k.op("pe", "matmul", [Nc[cur].sub(g), Lc[cur].sub(g)], [PB[5]], PB[5][:, hh * 128:(hh + 1) * 128], Nc[cur][:, h, :],
                             Lc[cur][:, h, :], start=True, stop=True)
                    k.op("act", "activation", [PB[5]], [Lc[nxt].sub(g)], out=Lc[nxt][:, hs, :].rearrange("p h j -> p (h j)"),
                         in_=PB[5][:, 0:384], func=AF.Copy)
                    if lvl < 5:
                        for hh in range(3):
                            h = 3 * g + hh
                            k.op("pe", "matmul", [Nc[cur].sub(g), Lc[cur].sub(g)], [PB[6]], PB[6][:, hh * 128:(hh + 1) * 128], Lc[cur][:, h, :],
                                 Nc[cur][:, h, :], start=True, stop=True)
                        k.op("dve", "tensor_copy", [PB[6]], [Nc[nxt].sub(g)], out=Nc[nxt][:, hs, :].rearrange("p h j -> p (h j)"),
                             in_=PB[6][:, 0:384])
                    for hh in range(3):
                        h = 3 * g + hh
                        k.op("pe", "matmul", [Lc[nxt].sub(g), Yb.sub(g)], [PB[7]], PB[7][:, hh * 128:(hh + 1) * 128], Lc[nxt][:, h, :],
                             Yb[:, h, :], start=True, stop=True)
                    k.op("dve", "tensor_tensor", [Yf.sub(g), PB[7]], [Yf.sub(g)], out=Yf[:, hs, :].rearrange("p h j -> p (h j)"),
                         in0=Yf[:, hs, :].rearrange("p h j -> p (h j)"), in1=PB[7][:, 0:384], op=ALU.add)
                    k.op("act", "activation", [Yf.sub(g)], [Yb.sub(g)], out=Yb[:, hs, :], in_=Yf[:, hs, :], func=AF.Copy)
                cur = nxt
            tp2 = PB[2]
            tp2b = tp2[:].bitcast(BF16)
            for h in range(NH):
                k.op("pe", "transpose", [kq.sub(6 + h), c.ident_b], [tp2], tp2b[:, h * 128:(h + 1) * 128], kq[:, 6 + h, bs], c.ident_b[:])
            k.op("dve", "tensor_tensor", [tp2, sv], [Xk], out=Xk[:], in0=tp2b[:, 0:768].rearrange("p (h d) -> p h d", h=NH),
                 in1=sv[:, 18:24].unsqueeze(2).to_broadcast([128, NH, 128]), op=ALU.mult)
            for h in range(NH):
                k.op("act", "activation", [tp2, sv], [kd], out=kd[:, h, :], in_=tp2b[:, h * 128:(h + 1) * 128], func=AF.Copy,
                     scale=sv[:, 24 + h:25 + h])
            tp3 = PB[3]
            tp3b = tp3[:].bitcast(BF16)
            for h in range(NH):
                k.op("pe", "transpose", [kq.sub(12 + h), c.ident_b], [tp3], tp3b[:, h * 128:(h + 1) * 128], kq[:, 12 + h, bs], c.ident_b[:])
            k.op("dve", "tensor_tensor", [tp3, gt], [Xv], out=Xv[:], in0=tp3b[:, 0:768].rearrange("p (h d) -> p h d", h=NH),
                 in1=gt[:, blk, 6:12].unsqueeze(2).to_broadcast([128, NH, 128]), op=ALU.mult)
            for g in range(2):
                hs = slice(3 * g, 3 * g + 3)
                for hh in range(3):
                    h = 3 * g + hh
                    k.op("pe", "matmul", [Yb.sub(g), Xv], [PB[5]], PB[5][:, hh * 128:(hh + 1) * 128], Yb[:, h, :], Xv[:, h, :],
                         start=True, stop=True)
                k.op("act", "activation", [PB[5]], [u_f.sub(g)], out=u_f[:, hs, :].rearrange("p h j -> p (h j)"), in_=PB[5][:, 0:384],
                     func=AF.Copy)
                for hh in range(3):
                    h = 3 * g + hh
                    k.op("pe", "matmul", [Yb.sub(g), Xk], [PB[6]], PB[6][:, hh * 128:(hh + 1) * 128], Xk[:, h, :], Yb[:, h, :],
                         start=True, stop=True)
                k.op("dve", "tensor_copy", [PB[6]], [wT.sub(g)], out=wT[:, hs, :].rearrange("p h j -> p (h j)"), in_=PB[6][:, 0:384])
            if full:
                for g in range(2):
                    hs = slice(3 * g, 3 * g + 3)
                    k.op("dve", "tensor_tensor", [kq.sub(3 * g), kq.sub(3 * g + 1), kq.sub(3 * g + 2), eGR[g]], [qdT.sub(g)],
                         out=qdT[:, hs, :], in0=kq[:, hs, bs], in1=eGr[:, hs, :], op=ALU.mult)
            for ch in range(2):
                cs = slice(ch * 64, ch * 64 + 64)
                last = ch * 64 + 63
                vb = [PB[2], PB[3], PB[4]] if not full else [PB[2], PB[3]]
                per = 2 if not full else 3
                for h in range(NH):
                    bank = vb[h // per]
                    o_ = (h % per) * NV
                    k.op("pe", "matmul", [wT.sub(h // 3), Sb[h]], [bank], bank[:, o_:o_ + NV], wT[:, h, :], S_b[:, h, :], start=True, stop=True)
                for h in range(NH):
                    bank = vb[h // per]
                    o_ = (h % per) * NV
                    k.op("dve", "scalar_tensor_tensor", [bank, u_f.sub(h // 3)], [vnew.sub(h)], out=vnew[cs, h, 0:128], in0=bank[cs, o_:o_ + 128],
                         scalar=-1.0, in1=u_f[cs, h, :], op0=ALU.mult, op1=ALU.add)
                    if not full:
                        k.op("act", "activation", [bank], [vnew.sub(h)], out=vnew[cs, h, 128:256], in_=bank[cs, o_ + 128:o_ + 256],
                             func=AF.Copy, scale=-1.0)
                if full:
                    ob = PB[4]
                    for h in range(NH):
                        k.op("pe", "matmul", [Sb[h], qdT.sub(h // 3)], [ob], ob[:, h * 64:(h + 1) * 64], S_b[:, h, :], qdT[:, h, cs],
                             start=True, stop=False)
                        k.op("pe", "matmul", [vnew.sub(h), ATb.sub(h // 3)], [ob], ob[:, h * 64:(h + 1) * 64], vnew[cs, h, 0:128],
                             ATb[cs, h, cs], start=False, stop=True)
                    k.op("act", "activation", [ob], [orawT.sub((blk, ch))], out=orawT[:, :, blk * 128 + ch * 64: blk * 128 + ch * 64 + 64],
                         in_=ob[:, 0:384].rearrange("p (h t) -> p h t", h=NH), func=AF.Copy)
                db = [PB[5], PB[6], PB[7]] if not full else [PB[5], PB[6]]
                for h in range(NH):
                    bank = db[h // per]
                    o_ = (h % per) * NV
                    k.op("pe", "matmul", [kd, vnew.sub(h)], [bank], bank[:, o_:o_ + NV], kd[cs, h, :], vnew[cs, h, :], start=True, stop=True)
                for h in range(NH):
                    bank = db[h // per]
                    o_ = (h % per) * NV
                    k.op("dve", "scalar_tensor_tensor", [Sf[h], eGR[h // 3], bank], [Sf[h]], out=S_f[:, h, :], in0=S_f[:, h, :],
                         scalar=eGr[:, h, last:last + 1], in1=bank[:, o_:o_ + NV], op0=ALU.mult, op1=ALU.add)
                    k.op("act", "activation", [Sf[h]], [Sb[h]], out=S_b[:, h, :], in_=S_f[:, h, :], func=AF.Copy)
        if not full:
            continue
        for h in range(NH):
            y, sq, r_ = ysil[h % 2], sqb[h % 2], rn[h % 2]
            rd = [orawT.sub((b_, c_)) for b_ in range(NBK) for c_ in range(2)]
            k.op("act", "activation", rd, [sq], out=sq[:], in_=orawT[:, h, :], func=AF.Square)
            p3 = next_ps(c)
            k.op("pe", "matmul", [ones_b, sq], [p3], p3[:, 0:TA], ones_b[:], sq[:], start=True, stop=True)
            k.op("act", "activation", [p3, small], [r_], out=r_[:], in_=p3[:, 0:TA], func=AF.Ln, bias=small[:, 18:19], scale=1.0 / 128)
            k.op("act", "activation", [r_], [r_], out=r_[:], in_=r_[:], func=AF.Exp, scale=-0.5)
            k.op("dve", "tensor_tensor", rd + [r_], [y], out=y[:], in0=orawT[:, h, :], in1=r_[:], op=ALU.mult)
            k.op("dve", "scalar_tensor_tensor", [y, normw, zs.sub(h)], [oT.sub((h, 0, 0))], out=oT[:, h, :], in0=y[:], scalar=normw[:, 0:1],
                 in1=zs[:, h, :], op0=ALU.mult, op1=ALU.mult)
        items = []
        for blk in range(NBK):
            qs = slice(blk * 128, (blk + 1) * 128)
            for hd in range(4):
                cc, pb = hd // 2, (hd % 2) * 64
                items.append(dict(q=(qmT.sub(cc), qmT[pb:pb + 64, cc, qs]),
                                  kT=(memKT, memKT[pb:pb + 64, cc, :]),
                                  v=(memV, [memV[:, kb, cc * 128:(cc + 1) * 128] for kb in range(2)]),
                                  mask=None, sink=None, scale=0.125,
                                  out=(oT.sub((6 + cc, blk, pb)), oT[pb:pb + 64, 6 + cc, qs]), pb=pb))
        attn_run(A, items)
        oT_reads = [oT.sub((h, 0, 0)) for h in range(6)] + [oT.sub((6 + cc, blk, pb)) for cc in range(2) for blk in range(4) for pb in (0, 64)]
        r = h_f
        for d in range(8):
            p = next_ps(c)
            for cc in range(8):
                rd = [w_o] + (oT_reads if (cc == 0 and d == 0) or (cc == 7 and d == 7) else [])
                k.op("pe", "matmul", rd, [p], p[:, 0:TA], w_o[:, cc, d * 128:(d + 1) * 128], oT[:, cc, :], start=(cc == 0), stop=(cc == 7))
            k.op("dve", "scalar_tensor_tensor", [h_f.sub(d), p], [r.sub(d)], out=r[:, d, :], in0=h_f[:, d, :], scalar=ALPHA,
                 in1=p[:, 0:TA], op0=ALU.mult, op1=ALU.add)
        ln_tile(c, r, lnp[:, li, 0, 0, :], lnp[:, li, 0, 1, :], tmp, lnp)
        k.dma("sp", dst_v[:, :, ts], r[:], reads=[r.sub(d) for d in range(8)], writes=[dst.sub(ti)], owner=r)
    if not full:
        k.dma("sp", L["pq_out"][:, :, :], S_f[:], reads=Sf, writes=[L["pq_out"]], owner=S_f)
    k.pop()
    c.pool = [0, 1, 2, 3]


def perm_w_in_a(w):
    out = np.zeros((w.shape[0], A_COLS), np.float32)
    out[:, 0:3072] = w[:, 0:3072]
    out[:, C_QM:C_QM + 256] = w[:, 3084:3340]
    out[:, C_AB:C_AB + 12] = w[:, 3072:3084]
    return out


def a_layer_inputs(inp, li):
    d = {}
    d["w_in"] = perm_w_in_a(inp["a_w_in"][li])
    cw = inp["a_conv_w"][li]
    d["convw"] = np.ascontiguousarray(cw.reshape(4, 18, 128).transpose(2, 1, 0))
    gc = np.concatenate([inp["a_dt_bias"][li], np.zeros(6, np.float32), inp["a_A_log"][li]]).astype(np.float32)
    d["gate_c"] = np.tile(gc[None, :], (128, 1))
    d["normw"] = np.ascontiguousarray(inp["a_norm_w"][li][:, None])
    d["w_o"] = inp["w_o"][li]
    d["mem_w_kv"] = inp["mem_w_kv"][li]
    return d


def build_a(T, li, mode):
    nc = bass.Bass("TRN2", target_bir_lowering=False)
    k = K(nc)
    inp = lambda n, s, dt=F32: k.dram(n, s, dt, "ExternalInput")
    hT = inp("hT", [D, T])
    G = {}
    cd = {"ident": inp("c_ident", [128, 128])}
    G["c_ident"] = cd["ident"]
    for n, a in make_consts_a().items():
        G["ca_" + n] = inp("ca_" + n, list(a.shape))
    G["memT"] = inp("memT", [D, 256])
    lnp_d = inp("lnp", [128, 4 * 2 * 2 * 8])
    L = dict(w_in=inp("w_in", [D, A_COLS]), convw=inp("convw", [128, 18, 4]), gate_c=inp("gate_c", [128, 18]),
             normw=inp("normw", [128, 1]), w_o=inp("w_o", [D, D]), mem_w_kv=inp("mem_w_kv", [D, 512]),
             hhalo=inp("hhalo", [D, 3]))
    c = setup_common(k, T, cd)
    lnp = k.sbuf("lnp_sb", [128, 4, 2, 2, 8], F32)
    k.dma("sp", lnp[:], lnp_d.t.rearrange("p (a b c d) -> p a b c d", a=4, b=2, c=2), writes=[lnp])
    G["lnp"] = lnp
    if mode == "full":
        pa = inp("pq_all", [8 * 128, 1536])
        L["pq_all_buf"] = pa
        L["pq_all"] = lambda r_, pa=pa: pa.t[r_ * 128:(r_ + 1) * 128, :].rearrange("p (h n) -> p h n", h=6)
        L["fold_ranks"] = [0, 1, 2, 4, 5, 6]
        L["gate_col"] = lambda r_: slice(r_, r_ + 1)
        gd = inp("gates", [128, 16])
        G["gates"] = k.sbuf("gates_sb", [128, 16], F32)
        k.dma("sp", G["gates"][:], gd[:, :], writes=[G["gates"]])
        outT = k.dram("outT", [D, T], F32, "ExternalOutput")
        mixer_a_phase(c, hT, outT, G, L, li, "full")
        k.out_events.append(list(k.all_dma.values())[-1])
    else:
        L["pq_out"] = k.dram("pq", [128, 6, 256], F32, "ExternalOutput")
        mixer_a_phase(c, hT, None, G, L, li, "pq")
    k.barrier()
    k.finish()
    return nc


NSEG = 4


def shared_kv_phase(c, src, G, L):
    k = c.k
    k.push()
    wkv = k.sbuf("wkvs", [128, 8, 512], BF16)
    load_w(k, wkv, L["w_kv_dup"])
    h_b = k.sbuf("h_b", [128, 8, TT], BF16)
    pos_i = k.sbuf("pos_i", [128, TT], I32)
    cos_t = k.sbuf("cos_t", [128, TT], F32)
    sin_t = k.sbuf("sin_t", [128, TT], F32)
    rt = {"ang": k.sbuf("ang", [128, TT], F32), "kf": k.sbuf("kf", [128, TT], F32), "ki": k.sbuf("ki", [128, TT], I32),
          "gg": k.sbuf("gg", [128, TT], F32)}
    rs = {"qa": [k.sbuf(f"qa{i}", [128, TT], BF16) for i in range(2)],
          "t1": [rt["ang"], rt["kf"]], "t2": [rt["gg"], k.sbuf("t2b", [128, TT], F32)]}
    ko = k.sbuf("ko", [128, 2, TT], F32)
    vo = k.sbuf("vo", [128, 4, 256], F32)
    src_v = src.t.rearrange("(k p) t -> p k t", p=128)
    vsh_v = L["vsh"].t.rearrange("(b p) x -> p b x", p=128)
    for ti in range(c.NT):
        ts = slice(ti * TT, (ti + 1) * TT)
        k.dma("pool", h_b[:], src_v[:, :, ts], reads=[src.sub(ti)], writes=[h_b], owner=h_b)
        k.dma("sp", pos_i[:], G["pos"][:, ts], writes=[pos_i])
        rope_tables(c, pos_i, G["invf"], cos_t, sin_t, rt)
        for kv in range(2):
            p = proj_chunk(c, wkv, wkv, kv * 128, h_b)
            rope_chunk(c, p, G["rot_b"], cos_t, sin_t, rs, ko.sub(kv), ko[:, kv, :], kv)
        k.dma("sp", L["kshT"][:, :, ts], ko[:], reads=[ko.sub(0), ko.sub(1)], writes=[L["kshT"].sub(ti)], owner=ko, is_output=True)
        for blk in range(4):
            p = next_ps(c)
            for kk in range(8):
                k.op("pe", "matmul", [wkv, h_b], [p], p[:, 0:256], h_b[:, kk, blk * 128:(blk + 1) * 128], wkv[:, kk, 256:512],
                     start=(kk == 0), stop=(kk == 7))
            k.op("act", "activation", [p], [vo.sub(blk)], out=vo[:, blk, :], in_=p[:, 0:256], func=AF.Copy)
        k.dma("sp", vsh_v[:, ti * 4:(ti + 1) * 4, :], vo[:], reads=[vo.sub(b_) for b_ in range(4)], writes=[L["vsh"].sub(ti)], owner=vo,
              is_output=True)
    k.pop()


def _a_inputs(k, T, full):
    inp = lambda n, s, dt=F32: k.dram(n, s, dt, "ExternalInput")
    G = {}
    cd = {"ident": inp("c_ident", [128, 128])}
    G["c_ident"] = cd["ident"]
    for n, a in make_consts_a().items():
        G["ca_" + n] = inp("ca_" + n, list(a.shape))
    L = dict(w_in=inp("w_in", [D, A_COLS]), convw=inp("convw", [128, 18, 4]), gate_c=inp("gate_c", [128, 18]),
             hhalo=inp("hhalo", [D, 3]))
    if full:
        G["memT"] = inp("memT", [D, 256])
        L.update(normw=inp("normw", [128, 1]), w_o=inp("w_o", [D, D]), mem_w_kv=inp("mem_w_kv", [D, 512]))
        pa = inp("pq_all", [8 * 128, 1536])
        L["pq_all_buf"] = pa
        L["pq_all"] = lambda r_, pa=pa: pa.t[r_ * 128:(r_ + 1) * 128, :].rearrange("p (h n) -> p h n", h=6)
        L["fold_ranks"] = [0, 1, 2, 4, 5, 6]
        L["gate_col"] = lambda r_: slice(r_, r_ + 1)
        G["gates_d"] = inp("gates", [128, 16])
    return G, L, cd


def build_pq(T, li):
    nc = bass.Bass("TRN2", target_bir_lowering=False)
    k = K(nc)
    hT = k.dram("hT", [D, T], F32, "ExternalInput")
    G, L, cd = _a_inputs(k, T, False)
    c = setup_common(k, T, cd)
    L["pq_out"] = k.dram("pq", [128, 6, 256], F32, "ExternalOutput")
    mixer_a_phase(c, hT, None, G, L, li, "pq")
    k.finish()
    return nc


def build_full(T, li, with_kv):
    nc = bass.Bass("TRN2", target_bir_lowering=False)
    k = K(nc)
    inp = lambda n, s, dt=F32: k.dram(n, s, dt, "ExternalInput")
    hT = inp("hT", [D, T])
    G, L, cd = _a_inputs(k, T, True)
    lnp_d = inp("lnp", [128, 4 * 2 * 2 * 8])
    w_up, w_dn = inp("w_up", [D, HID]), inp("w_dn", [HID, D])
    c = setup_common(k, T, cd)
    lnp = k.sbuf("lnp_sb", [128, 4, 2, 2, 8], F32)
    k.dma("sp", lnp[:], lnp_d.t.rearrange("p (a b c d) -> p a b c d", a=4, b=2, c=2), writes=[lnp])
    G["lnp"] = lnp
    G["gates"] = k.sbuf("gates_sb", [128, 16], F32)
    k.dma("sp", G["gates"][:], G["gates_d"][:, :], writes=[G["gates"]])
    hmid = k.dram("hmid", [D, T], F32, "Internal")
    outT = k.dram("outT", [D, T], F32, "ExternalOutput")
    mixer_a_phase(c, hT, hmid, G, L, li, "full")
    mlp_phase(c, hmid, outT, w_up, w_dn, lnp, li, dst_is_output=True)
    if with_kv:
        G["pos"] = inp("pos", [128, T], I32)
        cr, ci = inp("c_rot", [128, 128]), inp("c_invf", [128, 1])
        G["rot_b"] = k.sbuf("rot_b", [128, 128], BF16)
        k.dma("pool", G["rot_b"][:], cr[:, :], writes=[G["rot_b"]])
        G["invf"] = k.sbuf("invf", [128, 1], F32)
        k.dma("sp", G["invf"][:], ci[:, :], writes=[G["invf"]])
        L["w_kv_dup"] = inp("w_kv_dup", [D, 512])
        L["kshT"] = k.dram("kshT", [128, 2, T], F32, "ExternalOutput")
        L["vsh"] = k.dram("vsh", [T, 256], F32, "ExternalOutput")
        shared_kv_phase(c, outT, G, L)
    k.finish()
    return nc


def _lnp_host(inp):
    lnp = np.stack([np.asarray(inp["ln_g"]), np.asarray(inp["ln_b"])], axis=2)
    return np.ascontiguousarray(lnp.reshape(4, 2, 2, 8, 128).transpose(4, 0, 1, 2, 3).reshape(128, -1)).astype(np.float32)


def kernel(**inp):
    inp = {k_: np.asarray(v_) for k_, v_ in inp.items()}
    x, mem, positions = inp["x"], inp["mem"], inp["positions"]
    B, S, _ = x.shape
    T = S // NSEG
    NC = B * NSEG
    cores = list(range(NC))
    bs = [(c_ // NSEG, c_ % NSEG) for c_ in cores]
    ca = make_consts_a()
    cb = make_consts()
    lnp = _lnp_host(inp)
    ident = np.eye(128, dtype=np.float32)
    memT = [np.ascontiguousarray(mem[b].T) for b in range(B)]
    zeros_pq = np.zeros((128, 6, 256), np.float32)

    def halo3(hTs, c_):
        b, s = bs[c_]
        return np.zeros((D, 3), np.float32) if s == 0 else np.ascontiguousarray(hTs[c_ - 1][:, -3:])

    def a_common(c_, li, hTs):
        d = {"hT": hTs[c_], "hhalo": halo3(hTs, c_), "c_ident": ident}
        for n, a in ca.items():
            d["ca_" + n] = a
        al = a_layer_inputs(inp, li)
        for n in ("w_in", "convw", "gate_c"):
            d[n] = al[n]
        return d, al

    hTs = [np.ascontiguousarray(x[b, s * T:(s + 1) * T, :].T) for (b, s) in bs]
    kv = None
    for li in range(2):
        nc1 = build_pq(T, li)
        maps = [a_common(c_, li, hTs)[0] for c_ in cores]
        r1 = run_bass_kernel_spmd(nc1, maps, core_ids=cores)
        pqs = [r1.results[c_]["pq"] for c_ in cores]
        pq_all = np.zeros((8 * 128, 1536), np.float32)
        for c_ in cores:
            pq_all[c_ * 128:(c_ + 1) * 128] = pqs[c_].reshape(128, 1536)
        nc2 = build_full(T, li, with_kv=(li == 1))
        maps = []
        for c_ in cores:
            b, s = bs[c_]
            d, al = a_common(c_, li, hTs)
            d.update(memT=memT[b], normw=al["normw"], w_o=al["w_o"], mem_w_kv=al["mem_w_kv"], lnp=lnp,
                     w_up=inp["mlp_w_up"][li], w_dn=inp["mlp_w_down"][li])
            d["pq_all"] = pq_all
            g = np.zeros((128, 16), np.float32)
            for r_ in cores:
                rb, rs_ = bs[r_]
                if rb == b and rs_ < s:
                    g[:, r_] = 1.0
            d["gates"] = g
            if li == 1:
                wk = inp["w_kv_shared"]
                Kc, Vc = wk[:, 0:128], wk[:, 128:256]
                dup = lambda m: np.concatenate([m[:, 0:64], m[:, 0:64], m[:, 64:128], m[:, 64:128]], axis=1)
                d["w_kv_dup"] = np.ascontiguousarray(np.concatenate([dup(Kc), dup(Vc)], axis=1))
                d["pos"] = np.ascontiguousarray(np.tile(positions[b, s * T:(s + 1) * T][None, :], (128, 1))).astype(np.int32)
                d["c_rot"], d["c_invf"] = cb["rot"], cb["invf"]
            maps.append(d)
        r2 = run_bass_kernel_spmd(nc2, maps, core_ids=cores)
        hTs = [r2.results[c_]["outT"] for c_ in cores]
        if li == 1:
            kv = [(r2.results[c_]["kshT"], r2.results[c_]["vsh"]) for c_ in cores]
    nc5 = build_b(T)
    maps = []
    for c_ in cores:
        b, s = bs[c_]
        kT = np.zeros((128, 2, 128 + T), np.float32)
        v = np.zeros((128 + T, 2, 128), np.float32)
        kT[:, :, 128:] = kv[c_][0]
        v[128:] = kv[c_][1].reshape(T, 2, 128)
        if s > 0:
            kT[:, :, :128] = kv[c_ - 1][0][:, :, -128:]
            v[:128] = kv[c_ - 1][1][-128:].reshape(128, 2, 128)
        d = {"hT": hTs[c_], "pos": np.ascontiguousarray(np.tile(positions[b, s * T:(s + 1) * T][None, :], (128, 1))).astype(np.int32),
             "kT": kT, "v": v, "memT": memT[b], "sinks": np.tile(inp["b_sinks"].reshape(1, 24), (128, 1)).astype(np.float32), "lnp": lnp}
        for n, a in cb.items():
            d["c_" + n] = a
        if s > 0:
            d["c_mask_first"] = cb["mask_band"]
        for li in (2, 3):
            d[f"w_in{li}"] = inp["b_w_in"][li - 2]
            d[f"w_o{li}"] = inp["w_o"][li]
            d[f"mwkv{li}"] = inp["mem_w_kv"][li]
            d[f"w_up{li}"] = inp["mlp_w_up"][li]
            d[f"w_dn{li}"] = inp["mlp_w_down"][li]
        maps.append(d)
    r5 = run_bass_kernel_spmd(nc5, maps, core_ids=cores)
    out = np.zeros((B, S, D), np.float32)
    for c_ in cores:
        b, s = bs[c_]
        out[b, s * T:(s + 1) * T, :] = r5.results[c_]["outT"].T
    return out
```

```python
import contextlib
import numpy as np
import concourse.bass as bass
import concourse.mybir as mybir
from concourse.bass_utils import run_bass_kernel_spmd

F32 = mybir.dt.float32
BF16 = mybir.dt.bfloat16
I32 = mybir.dt.int32
AF = mybir.ActivationFunctionType
ALU = mybir.AluOpType
AX = mybir.AxisListType

SAME_SYNC = True


class Buf:
    def __init__(self, name, t=None):
        self.name = name
        self.t = t
        self.last_write = None
        self.readers = {}
        self.dsem = {}
        self.dcount = {}
        self.subs = {}
        self.excl = False

    def __getitem__(self, key):
        return self.t[key]

    def sub(self, key):
        if self.excl:
            return self
        if key not in self.subs:
            b = Buf(f"{self.name}.{key}", self.t)
            self.subs[key] = b
        return self.subs[key]


class K:
    ENG = ("pe", "act", "dve", "pool", "sp")

    def __init__(self, nc):
        self.nc = nc
        self.stack = contextlib.ExitStack()
        self.prog = {e: [] for e in self.ENG}
        self.count = {e: 0 for e in self.ENG}
        self.known = {e: {} for e in self.ENG}
        self.esem = {e: self.stack.enter_context(nc.semaphore("es_" + e)) for e in self.ENG}
        self.nsem = 0
        self.out_events = []
        self.all_dma = {}
        self.free_sems = {}
        self.scopes = []

    def new_sem(self, name):
        self.nsem += 1
        return self.stack.enter_context(self.nc.semaphore(f"ds{self.nsem}_{name}"))

    def push(self):
        st = contextlib.ExitStack()
        st.bufs = []
        self.scopes.append(st)

    def pop(self):
        self.barrier()
        st = self.scopes.pop()
        for b in st.bufs:
            for bb in [b] + list(b.subs.values()):
                for q, sm in bb.dsem.items():
                    self.free_sems.setdefault(q, []).append((sm, bb.dcount[q]))
                bb.dsem = {}
        st.close()

    def sbuf(self, name, shape, dtype):
        st = self.scopes[-1] if self.scopes else self.stack
        self.ntens = getattr(self, "ntens", 0) + 1
        t = st.enter_context(self.nc.sbuf_tensor(f"{name}_{self.ntens}", list(shape), dtype))
        b = Buf(name, t)
        if self.scopes:
            st.bufs.append(b)
        return b

    def psum(self, name, shape, dtype):
        t = self.stack.enter_context(self.nc.psum_tensor(name, list(shape), dtype))
        b = Buf(name, t)
        b.excl = True
        return b

    def dram(self, name, shape, dtype, kind):
        t = self.nc.dram_tensor(name, list(shape), dtype, kind=kind)
        return Buf(name, t.ap())

    def _wait(self, en, ev):
        if ev[0] == "eng":
            _, x, n = ev
            if x == en and (en == "pe" or not SAME_SYNC):
                return
            if self.known[en].get(x, 0) >= n:
                return
            self.known[en][x] = n
            s = self.esem[x]
            self.prog[en].append(lambda e, s=s, n=n: e.wait_ge(s, n))
        else:
            _, s, n = ev
            key = ("d", id(s))
            if self.known[en].get(key, 0) >= n:
                return
            self.known[en][key] = n
            self.prog[en].append(lambda e, s=s, n=n: e.wait_ge(s, n))

    def _deps(self, en, reads, writes):
        evs = []
        for r in reads:
            if r.last_write is not None:
                evs.append(r.last_write)
            if r.excl:
                evs.extend(v for kk, v in r.readers.items() if kk != en)
        for w in writes:
            if w.last_write is not None:
                evs.append(w.last_write)
            evs.extend(w.readers.values())
        for ev in evs:
            self._wait(en, ev)

    def op(self, en, meth, reads, writes, *args, **kwargs):
        fn = lambda e: getattr(e, meth)(*args, **kwargs)
        self._deps(en, reads, writes)
        self.count[en] += 1
        idx = self.count[en]
        s = self.esem[en]
        self.prog[en].append(lambda e, fn=fn, s=s: fn(e).then_inc(s, 1))
        ev = ("eng", en, idx)
        for r in reads:
            r.readers[en] = ev
        for w in writes:
            w.last_write = ev
            w.readers = {}
        return ev

    def dma(self, en, out, in_, reads=(), writes=(), owner=None, is_output=False, **kw):
        self._deps(en, reads, writes)
        if owner is None:
            owner = writes[0] if writes else reads[0]
        if en not in owner.dsem:
            if self.free_sems.get(en):
                owner.dsem[en], owner.dcount[en] = self.free_sems[en].pop()
            else:
                owner.dsem[en] = self.new_sem("d")
                owner.dcount[en] = 0
        owner.dcount[en] += 16
        s, n = owner.dsem[en], owner.dcount[en]
        self.prog[en].append(lambda e, s=s: e.dma_start(out=out, in_=in_, **kw).then_inc(s, 16))
        ev = ("dma", s, n)
        self.all_dma[id(s)] = ev
        for r in reads:
            r.readers[("d", id(s))] = ev
        for w in writes:
            w.last_write = ev
            w.readers = {}
        if is_output:
            self.out_events.append(ev)
        return ev

    def collective(self, in_buf, out_buf, n_ranks, i_ap=None, o_ap=None):
        en = "pool"
        self._deps(en, [in_buf], [out_buf])
        s = self.new_sem("cc")
        i_ap = in_buf.t if i_ap is None else i_ap
        o_ap = out_buf.t if o_ap is None else o_ap
        self.prog[en].append(lambda e, s=s: e.collective_compute(
            "AllGather", mybir.AluOpType.bypass, replica_groups=[list(range(n_ranks))],
            ins=[i_ap.opt()], outs=[o_ap.opt()]).then_inc(s, 1))
        ev = ("dma", s, 1)
        self.all_dma[id(s)] = ev
        in_buf.readers[("d", id(s))] = ev
        out_buf.last_write = ev
        out_buf.readers = {}
        return ev

    def finish(self, en="sp"):
        for ev in self.out_events:
            self._wait(en, ev)
        nc = self.nc
        with nc.Block() as block:
            m = {"pe": block.tensor, "act": block.scalar, "dve": block.vector,
                 "pool": block.gpsimd, "sp": block.sync}
            for en in self.ENG:
                plist = self.prog[en]
                def body(e, plist=plist):
                    for f in plist:
                        f(e)
                m[en](body)
        self.stack.close()


def _k_barrier(self):
    evs = [("eng", x, self.count[x]) for x in self.ENG if self.count[x] > 0]
    evs += list(self.all_dma.values())
    for en in self.ENG:
        for ev in evs:
            if ev[0] == "eng" and ev[1] == en:
                n = ev[2]
                if self.known[en].get(en, 0) < n:
                    self.known[en][en] = n
                    s = self.esem[en]
                    self.prog[en].append(lambda e, s=s, n=n: e.wait_ge(s, n))
            else:
                self._wait(en, ev)


K.barrier = _k_barrier

import math

D = 1024
HID = 4096
TT = 512
ALPHA = 8.0 ** 0.25
LN_EPS = 1e-5
NEG = -30000.0
C1 = 6.28125
C2 = 2 * math.pi - C1


class Ctx:
    pass


def setup_common(k, T, consts):
    c = Ctx()
    c.k = k
    c.T = T
    c.NT = T // TT
    c.ones_f = k.sbuf("ones_f", [128, 128], F32)
    k.op("dve", "memset", [], [c.ones_f], c.ones_f[:], 1.0 / D)
    c.ident_b = k.sbuf("ident_b", [128, 128], BF16)
    k.dma("pool", c.ident_b[:], consts["ident"][:, :], writes=[c.ident_b])
    c.ps = [k.psum(f"ps{i}", [128, 512], F32) for i in range(8)]
    c.psi = 0
    c.pool = [0, 1, 2, 3]
    return c


def next_ps(c):
    p = c.ps[c.pool[c.psi % len(c.pool)]]
    c.psi += 1
    return p


def ln_tile(c, r, gcol, bcol, tmp, lnbuf):
    k = c.k
    ones_f = c.ones_f
    ps_mean_b, ps_msq_b = next_ps(c), next_ps(c)
    w_ = tmp["w"]

    class _V:
        def __init__(self, b):
            self.b = b

        def __getitem__(self, key):
            return self.b[:, 0:w_]
    mean_sb, rstd, nmr = (tmp[n] for n in ("mean_sb", "rstd", "nmr"))
    sqs = [tmp["sq0"], tmp["sq1"]]
    for d in range(8):
        sq = sqs[d % 2]
        k.op("act", "activation", [r.sub(d)], [sq], out=sq[:], in_=r[:, d, :], func=AF.Square)
        k.op("pe", "matmul", [r.sub(d), ones_f], [ps_mean_b], ps_mean_b[:, 0:w_], ones_f[:], r[:, d, :], start=(d == 0), stop=(d == 7))
        k.op("pe", "matmul", [sq, ones_f], [ps_msq_b], ps_msq_b[:, 0:w_], ones_f[:], sq[:], start=(d == 0), stop=(d == 7))
    k.op("act", "activation", [ps_mean_b], [mean_sb], out=mean_sb[:], in_=ps_mean_b[:, 0:w_], func=AF.Copy)
    k.op("act", "activation", [ps_mean_b], [rstd], out=rstd[:], in_=ps_mean_b[:, 0:w_], func=AF.Square)
    k.op("dve", "tensor_tensor", [ps_msq_b, rstd], [rstd], out=rstd[:], in0=ps_msq_b[:, 0:w_], in1=rstd[:], op=ALU.subtract)
    k.op("dve", "tensor_scalar_add", [rstd], [rstd], out=rstd[:], in0=rstd[:], scalar1=LN_EPS)
    k.op("act", "activation", [rstd], [rstd], out=rstd[:], in_=rstd[:], func=AF.Ln)
    k.op("act", "activation", [rstd], [rstd], out=rstd[:], in_=rstd[:], func=AF.Exp, scale=-0.5)
    k.op("dve", "scalar_tensor_tensor", [mean_sb, rstd], [nmr], out=nmr[:], in0=mean_sb[:], scalar=-1.0, in1=rstd[:],
         op0=ALU.mult, op1=ALU.mult)
    for d in range(8):
        k.op("dve", "tensor_tensor", [r.sub(d), rstd], [r.sub(d)], out=r[:, d, :], in0=r[:, d, :], in1=rstd[:], op=ALU.mult)
        k.op("pool", "tensor_tensor", [r.sub(d), nmr], [r.sub(d)], out=r[:, d, :], in0=r[:, d, :], in1=nmr[:], op=ALU.add)
        k.op("act", "activation", [r.sub(d), lnbuf], [r.sub(d)], out=r[:, d, :], in_=r[:, d, :], func=AF.Identity,
             scale=gcol[:, d:d + 1], bias=bcol[:, d:d + 1])


def ln_tmp(k, tt=TT):
    d = {n: k.sbuf(n, [128, tt], F32) for n in ("sq0", "sq1", "mean_sb", "rstd", "nmr")}
    d["w"] = tt
    return d


def mlp_phase(c, src, dst, wup, wdn, lnp, li, dst_is_output=False):
    k = c.k
    k.push()
    c.pool = list(range(8))
    wup_sb = k.sbuf("wup_sb", [128, 8, HID], BF16)
    wdn_sb = k.sbuf("wdn_sb", [128, 32, D], BF16)
    h_f = k.sbuf("h_f", [128, 8, TT], F32)
    h_bs = [k.sbuf("h_b0", [128, 8, TT], BF16)] * 2
    act = k.sbuf("act", [128, 32, TT], BF16)
    rl = [k.sbuf(f"rl{i}", [128, TT], BF16) for i in range(2)]
    tmp = ln_tmp(k)
    r = h_f
    wup_v = wup.t.rearrange("(k p) n -> p k n", p=128)
    wdn_v = wdn.t.rearrange("(k p) n -> p k n", p=128)
    for kk in range(8):
        k.dma("pool", wup_sb[:, kk, :], wup_v[:, kk, :], writes=[wup_sb], owner=wup_sb)
    for kk in range(32):
        k.dma("pool", wdn_sb[:, kk, :], wdn_v[:, kk, :], writes=[wdn_sb], owner=wdn_sb)
    src_v = src.t.rearrange("(k p) t -> p k t", p=128)
    dst_v = dst.t.rearrange("(k p) t -> p k t", p=128)
    gcol = lnp[:, li, 1, 0, :]
    bcol = lnp[:, li, 1, 1, :]
    for ti in range(c.NT):
        ts = slice(ti * TT, (ti + 1) * TT)
        h_b = h_bs[ti % 2]
        k.dma("sp", h_f[:], src_v[:, :, ts], reads=[src.sub(ti)], writes=[h_f.sub(d) for d in range(8)], owner=h_f)
        k.dma("pool", h_b[:], src_v[:, :, ts], reads=[src.sub(ti)], writes=[h_b], owner=h_b)
        for cc in range(32):
            p = next_ps(c)
            for kk in range(8):
                k.op("pe", "matmul", [wup_sb, h_b], [p], p[:], wup_sb[:, kk, cc * 128:(cc + 1) * 128], h_b[:, kk, :],
                     start=(kk == 0), stop=(kk == 7))
            rb = rl[cc % 2]
            k.op("act", "activation", [p], [rb], out=rb[:], in_=p[:], func=AF.Relu)
            k.op("pool", "tensor_tensor", [rb], [act.sub(cc)], out=act[:, cc, :], in0=rb[:], in1=rb[:], op=ALU.mult)
        for d in range(8):
            p = next_ps(c)
            for cc in range(32):
                k.op("pe", "matmul", [wdn_sb, act.sub(cc)], [p], p[:], wdn_sb[:, cc, d * 128:(d + 1) * 128], act[:, cc, :],
                     start=(cc == 0), stop=(cc == 31))
            k.op("dve", "scalar_tensor_tensor", [h_f.sub(d), p], [r.sub(d)], out=r[:, d, :], in0=h_f[:, d, :], scalar=ALPHA,
                 in1=p[:], op0=ALU.mult, op1=ALU.add)
        ln_tile(c, r, gcol, bcol, tmp, lnp)
        k.dma("sp", dst_v[:, :, ts], r[:], reads=[r.sub(d) for d in range(8)], writes=[dst.sub(ti)], owner=r,
              is_output=dst_is_output)
    k.pop()
    c.pool = [0, 1, 2, 3]


class AttnRes:
    def __init__(self, c):
        k = c.k
        self.c = c
        self.n = 0
        self.s_ps = [(c.ps[4], 0), (c.ps[5], 0)]
        self.pT_ps = [(c.ps[6], 0), (c.ps[2], 0)]
        self.o_ps = [(c.ps[7], 0), (c.ps[3], 0)]
        self.negone = k.sbuf("negone", [128, 1], F32)
        k.op("dve", "memset", [], [self.negone], self.negone[:], -1.0)
        self.sm = [k.sbuf(f"sm{i}", [128, 256], F32) for i in range(2)]
        self.p = [k.sbuf(f"p{i}", [128, 256], BF16) for i in range(3)]
        self.dg = [k.sbuf(f"dg{i}", [128, 128], BF16) for i in range(3)]
        self.pT = [k.sbuf(f"pT{i}", [128, 256], BF16) for i in range(3)]
        self.st = [k.sbuf(f"st{i}", [128, 8], F32) for i in range(4)]


def attn_s0(A, it):
    k = A.c.k
    i = A.n
    A.n += 1
    it["i"] = i
    pbuf, off = A.s_ps[i % 2]
    sp = pbuf.sub(("s", off))
    it["s_ps"] = (sp, pbuf[:, off:off + 256])
    qb, qap = it["q"]
    kb, kap = it["kT"]
    k.op("pe", "matmul", [qb, kb], [sp], it["s_ps"][1], qap, kap, start=True, stop=True)


def attn_s1a(A, it):
    k = A.c.k
    i = it["i"]
    sp, sap = it["s_ps"]
    st = A.st[i % 4]
    scale = it["scale"]
    if it["mask"] is not None:
        sm = A.sm[i % 2]
        mb = it["mask"]
        k.op("dve", "scalar_tensor_tensor", [sp, mb], [sm], out=sm[:], in0=sap, scalar=scale, in1=mb[:],
             op0=ALU.mult, op1=ALU.add)
        it["src"] = (sm, sm[:], 1.0)
    else:
        it["src"] = (sp, sap, scale)
    src_b, src_ap, esc = it["src"]
    k.op("dve", "memset", [], [st], st[:, 2:3], 0.0)
    k.op("dve", "reduce_max", [src_b], [st], out=st[:, 0:1], in_=src_ap, axis=AX.X)
    if it["sink"] is not None:
        sb, sk = it["sink"]
        k.op("dve", "scalar_tensor_tensor", [st, sb, A.negone], [st], out=st[:, 1:2], in0=st[:, 0:1], scalar=sk, in1=A.negone[:, 0:1],
             op0=ALU.max, op1=ALU.mult)
    else:
        k.op("dve", "tensor_scalar", [st], [st], out=st[:, 1:2], in0=st[:, 0:1], scalar1=-esc, scalar2=None, op0=ALU.mult)


def attn_s1b(A, it):
    k = A.c.k
    i = it["i"]
    st = A.st[i % 4]
    p = A.p[i % 3]
    src_b, src_ap, esc = it["src"]
    k.op("act", "activation", [src_b, st], [p, st], out=p[:], in_=src_ap, func=AF.Exp, bias=st[:, 1:2], scale=esc,
         accum_out=st[:, 2:3])
    if it["sink"] is not None:
        sb, sk = it["sink"]
        k.op("act", "activation", [sb, st], [st], out=st[:, 3:4], in_=sk, func=AF.Exp, bias=st[:, 1:2], scale=1.0)


def attn_s1c(A, it):
    k = A.c.k
    i = it["i"]
    st = A.st[i % 4]
    dg = A.dg[i % 3]
    if it["sink"] is not None:
        k.op("dve", "tensor_tensor", [st], [st], out=st[:, 2:3], in0=st[:, 2:3], in1=st[:, 3:4], op=ALU.add)
    k.op("dve", "reciprocal", [st], [st], out=st[:, 4:5], in_=st[:, 2:3])
    k.op("pool", "tensor_tensor", [A.c.ident_b, st], [dg], out=dg[:], in0=A.c.ident_b[:], in1=st[:, 4:5].to_broadcast([128, 128]),
         op=ALU.mult)


def attn_s2(A, it):
    k = A.c.k
    i = it["i"]
    p = A.p[i % 3]
    dg = A.dg[i % 3]
    pbuf, off = A.pT_ps[i % 2]
    tp = pbuf.sub(("t", off))
    for kb in range(2):
        k.op("pe", "matmul", [p, dg], [tp], pbuf[:, off + kb * 128: off + (kb + 1) * 128], p[:, kb * 128:(kb + 1) * 128], dg[:],
             start=True, stop=True)
    pT = A.pT[i % 3]
    k.op("act", "activation", [tp], [pT], out=pT[:], in_=pbuf[:, off:off + 256], func=AF.Copy)


def attn_s3(A, it):
    k = A.c.k
    i = it["i"]
    pT = A.pT[i % 3]
    pbuf, off = A.o_ps[i % 2]
    op_ = pbuf.sub(("o", off))
    vb, vaps = it["v"]
    for kb in range(2):
        k.op("pe", "matmul", [pT, vb], [op_], pbuf[:, off:off + 128], vaps[kb], pT[:, kb * 128:(kb + 1) * 128],
             start=(kb == 0), stop=(kb == 1))
    ob, oap = it["out"]
    pb = it["pb"]
    k.op("act", "activation", [op_], [ob], out=oap, in_=pbuf[pb:pb + 64, off:off + 128], func=AF.Copy)


def attn_run(A, items):
    n = len(items)
    for i in range(n + 3):
        if i < n:
            attn_s0(A, items[i])
            attn_s1a(A, items[i])
            attn_s1b(A, items[i])
        if 0 <= i - 1 < n:
            attn_s1c(A, items[i - 1])
        if 0 <= i - 2 < n:
            attn_s2(A, items[i - 2])
        if 0 <= i - 3 < n:
            attn_s3(A, items[i - 3])


def rope_tables(c, pos_i, invf, cos_t, sin_t, tmp):
    k = c.k
    ang, kf, ki, gg = tmp["ang"], tmp["kf"], tmp["ki"], tmp["gg"]
    k.op("dve", "tensor_copy", [pos_i], [ang], out=ang[:], in_=pos_i[:])
    k.op("dve", "tensor_scalar", [ang, invf], [ang], out=ang[:], in0=ang[:], scalar1=invf[:, 0:1], scalar2=None, op0=ALU.mult)
    for dst, shift in ((sin_t, 0.0), (cos_t, math.pi / 2)):
        if shift:
            k.op("dve", "tensor_scalar_add", [ang], [dst], out=dst[:], in0=ang[:], scalar1=shift)
            src = dst
        else:
            src = ang
        k.op("dve", "tensor_scalar", [src], [kf], out=kf[:], in0=src[:], scalar1=1.0 / (2 * math.pi), scalar2=None, op0=ALU.mult)
        k.op("dve", "tensor_copy", [kf], [ki], out=ki[:], in_=kf[:])
        k.op("dve", "tensor_copy", [ki], [kf], out=kf[:], in_=ki[:])
        k.op("dve", "scalar_tensor_tensor", [kf, src], [dst], out=dst[:], in0=kf[:], scalar=-C1, in1=src[:], op0=ALU.mult, op1=ALU.add)
        k.op("dve", "scalar_tensor_tensor", [kf, dst], [dst], out=dst[:], in0=kf[:], scalar=-C2, in1=dst[:], op0=ALU.mult, op1=ALU.add)
        k.op("dve", "tensor_scalar", [dst], [gg], out=gg[:], in0=dst[:], scalar1=math.pi, scalar2=None, op0=ALU.is_gt)
        k.op("dve", "scalar_tensor_tensor", [gg, dst], [dst], out=dst[:], in0=gg[:], scalar=-2 * math.pi, in1=dst[:], op0=ALU.mult, op1=ALU.add)
        k.op("act", "activation", [dst], [dst], out=dst[:], in_=dst[:], func=AF.Sin)


class PV:
    pass


def proj_chunk(c, w_sb, wbuf, col0, h_b, w=TT):
    k = c.k
    p = next_ps(c)
    for kk in range(8):
        k.op("pe", "matmul", [wbuf, h_b], [p], p[:, 0:w], w_sb[:, kk, col0:col0 + 128], h_b[:, kk, :], start=(kk == 0), stop=(kk == 7))
    return p


def rope_chunk(c, p, rot_b, cos_t, sin_t, rs, dst_buf, dst_ap, j):
    k = c.k
    qa, t1, t2 = rs["qa"][j % 2], rs["t1"][j % 2], rs["t2"][j % 2]
    k.op("act", "activation", [p], [qa], out=qa[:], in_=p[:], func=AF.Copy)
    p2 = next_ps(c)
    k.op("pe", "matmul", [rot_b, qa], [p2], p2[:], rot_b[:], qa[:], start=True, stop=True)
    k.op("dve", "tensor_tensor", [p, cos_t], [t1], out=t1[:], in0=p[:], in1=cos_t[:], op=ALU.mult)
    k.op("dve", "tensor_tensor", [p2, sin_t], [t2], out=t2[:], in0=p2[:], in1=sin_t[:], op=ALU.mult)
    k.op("pool", "tensor_tensor", [t1, t2], [dst_buf], out=dst_ap, in0=t1[:], in1=t2[:], op=ALU.add)


def load_w(k, dst, src_dram, nk=8, eng="pool"):
    v = src_dram.t.rearrange("(k p) n -> p k n", p=128)
    for kk in range(nk):
        k.dma(eng, dst[:, kk, :], v[:, kk, :], writes=[dst], owner=dst)


def mem_kv(c, memT_b, wkv_sb, memKT, memV):
    k = c.k
    for cc in range(2):
        p = next_ps(c)
        for kk in range(8):
            k.op("pe", "matmul", [wkv_sb, memT_b], [p], p[:, 0:256], wkv_sb[:, kk, cc * 128:(cc + 1) * 128], memT_b[:, kk, :],
                 start=(kk == 0), stop=(kk == 7))
        k.op("act", "activation", [p], [memKT], out=memKT[:, cc, :], in_=p[:, 0:256], func=AF.Copy)
    for kb in range(2):
        p = next_ps(c)
        for kk in range(8):
            k.op("pe", "matmul", [wkv_sb, memT_b], [p], p[:, 0:256], memT_b[:, kk, kb * 128:(kb + 1) * 128], wkv_sb[:, kk, 256:512],
                 start=(kk == 0), stop=(kk == 7))
        k.op("act", "activation", [p], [memV], out=memV[:, kb, :], in_=p[:, 0:256], func=AF.Copy)


def mixer_b_phase(c, src, dst, G, L, li):
    k = c.k
    j = li - 2
    k.push()
    c.pool = [0, 1]
    w_in = k.sbuf("w_in", [128, 8, 1024], BF16)
    w_o = k.sbuf("w_o", [128, 8, 1024], BF16)
    wkv = k.sbuf("wkv", [128, 8, 512], BF16)
    load_w(k, wkv, L["mem_w_kv"])
    load_w(k, w_in, L["w_in"])
    load_w(k, w_o, L["w_o"])
    memKT = k.sbuf("memKT", [128, 2, 256], BF16)
    memV = k.sbuf("memV", [128, 2, 256], BF16)
    memT_b = k.sbuf("memT_b", [128, 8, 256], BF16)
    k.dma("pool", memT_b[:], G["memT"].t.rearrange("(k p) m -> p k m", p=128), writes=[memT_b])
    mem_kv(c, memT_b, wkv, memKT, memV)
    NB = c.T // 128
    kT = k.sbuf("kT_sb", [128, 2, 128 + c.T], BF16)
    v = k.sbuf("v_sb", [128, NB + 1, 2, 128], BF16)
    if "kvt_all" in G:
        k.dma("pool", kT[:, :, 128:], G["kshT"][:, :, :], reads=[G["kshT"]], writes=[kT], owner=kT)
        k.dma("pool", v[:, 1:, :, :], G["vsh"].t.rearrange("(b p) (k d) -> p b k d", p=128, k=2), reads=[G["vsh"]], writes=[v], owner=v)
        k.push()
        kvt = k.sbuf("kvt", [128, 8, 512], F32)
        kacc = k.sbuf("kacc", [128, 512], F32)
        gates = G["gates"]
        k.dma("sp", kvt[:], G["kvt_all"].t.rearrange("(r p) x -> p r x", p=128), reads=[G["kvt_all"]], writes=[kvt])
        k.op("dve", "tensor_scalar", [kvt, gates], [kacc], out=kacc[:], in0=kvt[:, 0, :], scalar1=gates[:, 8:9], scalar2=None, op0=ALU.mult)
        for r_ in range(1, 8):
            k.op("dve", "scalar_tensor_tensor", [kvt, gates, kacc], [kacc], out=kacc[:], in0=kvt[:, r_, :], scalar=gates[:, 8 + r_:9 + r_],
                 in1=kacc[:], op0=ALU.mult, op1=ALU.add)
        k.op("dve", "tensor_copy", [kacc], [kT], out=kT[:, :, 0:128], in_=kacc[:, 0:256].rearrange("p (k t) -> p k t", k=2))
        k.op("dve", "tensor_copy", [kacc], [v], out=v[:, 0, :, :], in_=kacc[:, 256:512].rearrange("p (k d) -> p k d", k=2))
        k.pop()
    else:
        k.dma("pool", kT[:], G["kTd"][:, :, :], writes=[kT])
        k.dma("pool", v[:], G["vd"].t.rearrange("(b p) k d -> p b k d", p=128), writes=[v])
    h_f = k.sbuf("h_f", [128, 8, TT], F32)
    h_bs = [k.sbuf(f"h_b{i}", [128, 8, TT], BF16) for i in range(2)]
    pos_i = k.sbuf("pos_i", [128, TT], I32)
    cos_t = k.sbuf("cos_t", [128, TT], F32)
    sin_t = k.sbuf("sin_t", [128, TT], F32)
    rt = {"ang": k.sbuf("ang", [128, TT], F32), "kf": k.sbuf("kf", [128, TT], F32), "ki": k.sbuf("ki", [128, TT], I32),
          "gg": k.sbuf("gg", [128, TT], F32)}
    rs = {"qa": [k.sbuf(f"qa{i}", [128, TT], BF16) for i in range(2)],
          "t1": [rt["ang"], rt["kf"]], "t2": [rt["gg"], k.sbuf("t2b", [128, TT], F32)]}
    qT = k.sbuf("qT", [128, 8, TT], BF16)
    oT = k.sbuf("oT", [128, 8, TT], BF16)
    tmp = ln_tmp(k)
    A = AttnRes(c)
    src_v = src.t.rearrange("(k p) t -> p k t", p=128)
    dst_v = dst.t.rearrange("(k p) t -> p k t", p=128)
    lnp = G["lnp"]
    gcol = lnp[:, li, 0, 0, :]
    bcol = lnp[:, li, 0, 1, :]
    scale = 0.125
    for ti in range(c.NT):
        ts = slice(ti * TT, (ti + 1) * TT)
        h_b = h_bs[ti % 2]
        k.dma("sp", h_f[:], src_v[:, :, ts], reads=[src.sub(ti)], writes=[h_f.sub(d) for d in range(8)], owner=h_f)
        k.dma("pool", h_b[:], src_v[:, :, ts], reads=[src.sub(ti)], writes=[h_b], owner=h_b)
        k.dma("sp", pos_i[:], G["pos"][:, ts], writes=[pos_i])
        rope_tables(c, pos_i, G["invf"], cos_t, sin_t, rt)
        for cc in range(8):
            p = proj_chunk(c, w_in, w_in, cc * 128, h_b)
            if cc < 6:
                rope_chunk(c, p, G["rot_b"], cos_t, sin_t, rs, qT.sub(cc), qT[:, cc, :], cc)
            else:
                k.op("act", "activation", [p], [qT.sub(cc)], out=qT[:, cc, :], in_=p[:], func=AF.Copy)
        items = []
        for blk in range(4):
            gb = ti * 4 + blk
            qs = slice(blk * 128, (blk + 1) * 128)
            for hd in range(12):
                cc, pb = hd // 2, (hd % 2) * 64
                kv = hd // 6
                mask = G["mask_first"] if gb == 0 else G["mask_band"]
                items.append(dict(q=(qT.sub(cc), qT[pb:pb + 64, cc, qs]),
                                  kT=(kT, kT[pb:pb + 64, kv, gb * 128: gb * 128 + 256]),
                                  v=(v, [v[:, gb + kb, kv, :] for kb in range(2)]),
                                  mask=mask, sink=(G["sinks"], G["sinks"][:, j * 12 + hd: j * 12 + hd + 1]),
                                  scale=scale, out=(oT.sub((cc, blk, pb)), oT[pb:pb + 64, cc, qs]), pb=pb))
            for hd in range(4):
                cc, pb = 6 + hd // 2, (hd % 2) * 64
                mc = hd // 2
                items.append(dict(q=(qT.sub(cc), qT[pb:pb + 64, cc, qs]),
                                  kT=(memKT, memKT[pb:pb + 64, mc, :]),
                                  v=(memV, [memV[:, kb, mc * 128:(mc + 1) * 128] for kb in range(2)]),
                                  mask=None, sink=None, scale=scale,
                                  out=(oT.sub((cc, blk, pb)), oT[pb:pb + 64, cc, qs]), pb=pb))
        attn_run(A, items)
        oT_reads = [oT.sub((cc, blk, pb)) for cc in range(8) for blk in range(4) for pb in (0, 64)]
        r = h_f
        for d in range(8):
            p = next_ps(c)
            for cc in range(8):
                rd = [w_o] + (oT_reads if (cc == 0 and d == 0) or (cc == 7 and d == 7) else [])
                k.op("pe", "matmul", rd, [p], p[:], w_o[:, cc, d * 128:(d + 1) * 128], oT[:, cc, :], start=(cc == 0), stop=(cc == 7))
            k.op("dve", "scalar_tensor_tensor", [h_f.sub(d), p], [r.sub(d)], out=r[:, d, :], in0=h_f[:, d, :], scalar=ALPHA,
                 in1=p[:], op0=ALU.mult, op1=ALU.add)
        ln_tile(c, r, gcol, bcol, tmp, lnp)
        k.dma("sp", dst_v[:, :, ts], r[:], reads=[r.sub(d) for d in range(8)], writes=[dst.sub(ti)], owner=r)
    k.pop()
    c.pool = [0, 1, 2, 3]


def make_consts():
    ident = np.eye(128, dtype=np.float32)
    rot = np.zeros((128, 128), np.float32)
    for m in range(128):
        hb, i = (m // 64) * 64, m % 64
        if i < 32:
            rot[hb + i + 32, m] = -1.0
        else:
            rot[hb + i - 32, m] = 1.0
    qi = np.arange(128)[:, None]
    kj = np.arange(256)[None, :]
    diff = kj - qi
    band = (diff >= 1) & (diff <= 128)
    mask_band = np.where(band, 0.0, NEG).astype(np.float32)
    mask_first = np.where(band & (kj >= 128), 0.0, NEG).astype(np.float32)
    invf = (10000.0 ** (-(np.arange(0, 64, 2, dtype=np.float32)) / 64)).astype(np.float32)
    invf128 = invf[np.arange(128) % 32][:, None].astype(np.float32)
    return dict(ident=ident, rot=rot, mask_band=mask_band, mask_first=mask_first, invf=invf128)


def build_b(T, layers=(2, 3)):
    nc = bass.Bass("TRN2", target_bir_lowering=False)
    k = K(nc)
    NB = T // 128
    inp = lambda n, s, dt=F32: k.dram(n, s, dt, "ExternalInput")
    hT = inp("hT", [D, T])
    pos = inp("pos", [128, T], I32)
    kTd = inp("kT", [128, 2, 128 + T])
    vd = inp("v", [128 + T, 2, 128])
    memT = inp("memT", [D, 256])
    cd = {n: inp("c_" + n, list(a.shape)) for n, a in make_consts().items() if n != "mask_first"}
    cd["mask_first"] = inp("c_mask_first", [128, 256])
    sinks = inp("sinks", [128, 24])
    lnp_d = inp("lnp", [128, 4 * 2 * 2 * 8])
    Ls = {}
    for li in layers:
        Ls[li] = dict(w_in=inp(f"w_in{li}", [D, 1024]), w_o=inp(f"w_o{li}", [D, D]), mem_w_kv=inp(f"mwkv{li}", [D, 512]),
                      w_up=inp(f"w_up{li}", [D, HID]), w_dn=inp(f"w_dn{li}", [HID, D]))
    outT = k.dram("outT", [D, T], F32, "ExternalOutput")
    hmid = k.dram("hmid", [D, T], F32, "Internal")
    hnext = k.dram("hnext", [D, T], F32, "Internal")

    c = setup_common(k, T, cd)
    G = {}
    G["pos"] = pos
    G["kTd"], G["vd"], G["memT"] = kTd, vd, memT
    for n in ("mask_band", "mask_first"):
        G[n] = k.sbuf(n, [128, 256], F32)
        k.dma("sp", G[n][:], cd[n][:, :], writes=[G[n]])
    G["rot_b"] = k.sbuf("rot_b", [128, 128], BF16)
    k.dma("pool", G["rot_b"][:], cd["rot"][:, :], writes=[G["rot_b"]])
    G["invf"] = k.sbuf("invf", [128, 1], F32)
    k.dma("sp", G["invf"][:], cd["invf"][:, :], writes=[G["invf"]])
    G["sinks"] = k.sbuf("sinks_sb", [128, 24], F32)
    k.dma("sp", G["sinks"][:], sinks[:, :], writes=[G["sinks"]])
    lnp = k.sbuf("lnp_sb", [128, 4, 2, 2, 8], F32)
    k.dma("sp", lnp[:], lnp_d.t.rearrange("p (a b c d) -> p a b c d", a=4, b=2, c=2), writes=[lnp])
    G["lnp"] = lnp
    src = hT
    for n_, li in enumerate(layers):
        mixer_b_phase(c, src, hmid, G, Ls[li], li)
        last = (n_ == len(layers) - 1)
        dst = outT if last else hnext
        mlp_phase(c, hmid, dst, Ls[li]["w_up"], Ls[li]["w_dn"], lnp, li, dst_is_output=last)
        src = dst
    k.finish()
    return nc


DK = 128
NH = 6
BIG = 30000.0
NORM_EPS = 1e-6
C_Q, C_K, C_V, C_Z, C_QM, C_AB = 0, 768, 1536, 2304, 3072, 3328
A_COLS = 3344
TA = 256
NBK = TA // 128


def make_consts_a():
    p = np.arange(128)
    same = (p[:, None] // 64) == (p[None, :] // 64)
    U = (same & (p[:, None] <= p[None, :])).astype(np.float32)
    CH = same.astype(np.float32)
    ML = np.where(same & (p[:, None] > p[None, :]), 0.0, BIG).astype(np.float32)
    MA = np.where(same & (p[None, :] >= p[:, None]), 0.0, -BIG).astype(np.float32)
    ML3 = np.tile(ML[:, None, :], (1, 3, 1)).reshape(128, 384)
    MA3 = np.tile(MA[:, None, :], (1, 3, 1)).reshape(128, 384)
    ident3 = np.tile(np.eye(128, dtype=np.float32)[:, None, :], (1, 3, 1)).reshape(128, 384)
    return dict(U=U, CH=CH, ML3=ML3, MA3=MA3, ident3=ident3)


class ARes:
    pass


def mixer_a_phase(c, src, dst, G, L, li, mode):
    k = c.k
    full = (mode == "full")
    NV = 128 if full else 256
    T = c.T
    k.push()
    c.pool = [0, 1]
    PB = c.ps
    w_in = k.sbuf("w_in", [128, 8, A_COLS], BF16)
    load_w(k, w_in, L["w_in"])
    convw = k.sbuf("convw", [128, 18, 4], F32)
    k.dma("sp", convw[:], L["convw"][:, :, :], writes=[convw])
    dconv = k.sbuf("dconv", [128, 18, 4, 128], BF16)
    for cc in range(18):
        for j in range(4):
            k.op("dve", "tensor_scalar", [c.ident_b, convw], [dconv], out=dconv[:, cc, j, :], in0=c.ident_b[:],
                 scalar1=convw[:, cc, j:j + 1], scalar2=None, op0=ALU.mult)
    small = k.sbuf("small_a", [128, 32], F32)
    k.dma("sp", small[:, 0:18], L["gate_c"][:, :], writes=[small])
    k.op("act", "activation", [small], [small], out=small[:, 12:18], in_=small[:, 12:18], func=AF.Exp)
    k.op("dve", "tensor_scalar", [small], [small], out=small[:, 12:18], in0=small[:, 12:18], scalar1=-1.0, scalar2=None, op0=ALU.mult)
    k.op("dve", "memset", [], [small], small[:, 18:19], NORM_EPS)
    k.op("dve", "memset", [], [small], small[:, 19:20], 1.0)
    k.op("dve", "memset", [], [small], small[:, 20:21], -0.5 * math.log(128.0))
    k.op("dve", "memset", [], [small], small[:, 21:22], 0.0)
    ones_b = k.sbuf("ones_b", [128, 128], BF16)
    k.op("dve", "memset", [], [ones_b], ones_b[:], 1.0)
    ones1 = k.sbuf("ones1", [128, 128], F32)
    k.op("dve", "memset", [], [ones1], ones1[:], 1.0)
    cf = {}
    for n, w_ in (("U", 128), ("CH", 128), ("ML3", 384), ("MA3", 384), ("ident3", 384)):
        cf[n] = k.sbuf("c_" + n, [128, w_], F32)
        k.dma("sp", cf[n][:], G["ca_" + n][:, :], writes=[cf[n]])
    ident_f = k.sbuf("ident_f", [128, 128], F32)
    k.dma("sp", ident_f[:], G["c_ident"][:, :], writes=[ident_f])
    S_f = k.sbuf("S_f", [128, NH, NV], F32)
    S_b = k.sbuf("S_b", [128, NH, NV], BF16)
    if full:
        k.op("dve", "memset", [], [S_f], S_f[:], 0.0)
        k.push()
        pq_sb = k.sbuf("pq_sb", [128, NH, 256], F32)
        PT_sb = k.sbuf("PT_sb", [128, 128], F32)
        tf = k.sbuf("tf", [128, 128], F32)
        gates = G["gates"]
        for r_ in L["fold_ranks"]:
            k.dma("sp", pq_sb[:], L["pq_all"](r_), reads=[L["pq_all_buf"]], writes=[pq_sb], owner=pq_sb)
            for h in range(NH):
                tpp = next_ps(c)
                k.op("pe", "transpose", [pq_sb, ident_f], [tpp], tpp[:, 0:128], pq_sb[:, h, 128:256], ident_f[:])
                k.op("act", "activation", [tpp], [PT_sb], out=PT_sb[:], in_=tpp[:, 0:128], func=AF.Copy)
                np_ = next_ps(c)
                k.op("pe", "matmul", [PT_sb, S_f], [np_], np_[:, 0:128], PT_sb[:], S_f[:, h, :], start=True, stop=True)
                k.op("dve", "tensor_tensor", [np_, pq_sb], [tf], out=tf[:], in0=np_[:, 0:128], in1=pq_sb[:, h, 0:128], op=ALU.add)
                k.op("dve", "tensor_tensor", [tf, S_f], [tf], out=tf[:], in0=tf[:], in1=S_f[:, h, :], op=ALU.subtract)
                k.op("dve", "scalar_tensor_tensor", [tf, S_f, gates], [S_f], out=S_f[:, h, :], in0=tf[:], scalar=gates[:, L["gate_col"](r_)],
                     in1=S_f[:, h, :], op0=ALU.mult, op1=ALU.add)
        k.pop()
    else:
        k.op("dve", "memset", [], [S_f], S_f[:], 0.0)
        for h in range(NH):
            k.op("dve", "tensor_copy", [ident_f, S_f], [S_f], out=S_f[:, h, 128:256], in_=ident_f[:])
    k.op("act", "activation", [S_f], [S_b], out=S_b[:], in_=S_f[:], func=AF.Copy)
    Sf = [S_f.sub(h) for h in range(NH)]
    Sb = [S_b.sub(h) for h in range(NH)]
    for h in range(NH):
        Sf[h].last_write = S_f.last_write
        Sb[h].last_write = S_b.last_write
    h_b = k.sbuf("h_b", [128, 8, TA], BF16)
    hh_b = k.sbuf("hh_b", [128, 8, 3], BF16)
    halo = k.sbuf("halo", [128, 18, 3], BF16)
    xcs = [k.sbuf(f"xc{i}", [128, TA + 3], BF16) for i in range(3)]
    ystore = k.sbuf("ystore", [128, 12, TA], BF16)
    tmp = ln_tmp(k, TA)
    ysil = [tmp["sq0"], tmp["sq1"]]
    rn = [tmp["mean_sb"], tmp["rstd"]]
    sqb = [k.sbuf(f"sqb{i}", [128, TA], BF16) for i in range(2)]
    qkv = k.sbuf("qkv", [128, 18, TA], BF16)
    gt = k.sbuf("gt", [128, NBK, 32], F32)
    src_v = src.t.rearrange("(k p) t -> p k t", p=128)
    if full:
        h_f = k.sbuf("h_f", [128, 8, TA], F32)
        zs = k.sbuf("zs", [128, 6, TA], BF16)
        qmT = k.sbuf("qmT", [128, 2, TA], BF16)
        oT = k.sbuf("oT", [128, 8, TA], BF16)
        orawT = k.sbuf("orawT", [128, 6, TA], F32)
        w_o = k.sbuf("w_o", [128, 8, 1024], BF16)
        load_w(k, w_o, L["w_o"])
        memKT = k.sbuf("memKT", [128, 2, 256], BF16)
        memV = k.sbuf("memV", [128, 2, 256], BF16)
        k.push()
        wkv = k.sbuf("wkv", [128, 8, 512], BF16)
        load_w(k, wkv, L["mem_w_kv"])
        memT_b = k.sbuf("memT_b", [128, 8, 256], BF16)
        k.dma("pool", memT_b[:], G["memT"].t.rearrange("(k p) m -> p k m", p=128), writes=[memT_b])
        mem_kv(c, memT_b, wkv, memKT, memV)
        k.pop()
        normw = k.sbuf("normw", [128, 1], F32)
        k.dma("sp", normw[:], L["normw"][:, :], writes=[normw])
        A = AttnRes(c)
        dst_v = dst.t.rearrange("(k p) t -> p k t", p=128)
        lnp = G["lnp"]
    def B6(n, dt=BF16):
        return k.sbuf(n, [128, NH, 128], dt)
    sv = k.sbuf("sv", [128, 64], F32)
    Eb, ET, eGr = B6("Eb", F32), B6("ET", F32), B6("eGr", F32)
    Lc = [B6("Lc0"), B6("Lc1")]
    Nc = [B6("Nc0"), B6("Nc1")]
    Yf, Yb = B6("Yf", F32), B6("Yb")
    Dg = Yf
    ATb = B6("ATb")
    Xk, Xv, kd = B6("Xk"), B6("Xv"), B6("kd")
    u_f = B6("u_f", F32)
    wT = B6("wT")
    qdT = B6("qdT")
    vnew = k.sbuf("vnew", [128, NH, NV], BF16)

    if "hl_all" in L:
        k.push()
        hl = k.sbuf("hl", [128, 8, 8, 3], F32)
        hacc = k.sbuf("hacc", [128, 8, 3], F32)
        k.dma("sp", hl[:], L["hl_all"].t.rearrange("(r k p) t -> p r k t", r=8, p=128), reads=[L["hl_all"]], writes=[hl])
        gates = G["gates"]
        k.op("dve", "tensor_scalar", [hl, gates], [hacc], out=hacc[:], in0=hl[:, 0, :, :], scalar1=gates[:, 8:9], scalar2=None, op0=ALU.mult)
        for r_ in range(1, 8):
            k.op("dve", "scalar_tensor_tensor", [hl, gates, hacc], [hacc], out=hacc[:], in0=hl[:, r_, :, :], scalar=gates[:, 8 + r_:9 + r_],
                 in1=hacc[:], op0=ALU.mult, op1=ALU.add)
        k.op("dve", "tensor_copy", [hacc], [hh_b], out=hh_b[:], in_=hacc[:])
        k.pop()
    else:
        k.dma("pool", hh_b[:], L["hhalo"].t.rearrange("(k p) t -> p k t", p=128), writes=[hh_b])
    for cc in range(18):
        p = next_ps(c)
        for kk in range(8):
            k.op("pe", "matmul", [w_in, hh_b], [p], p[:, 0:3], w_in[:, kk, cc * 128:(cc + 1) * 128], hh_b[:, kk, :],
                 start=(kk == 0), stop=(kk == 7))
        k.op("act", "activation", [p], [halo.sub(cc)], out=halo[:, cc, :], in_=p[:, 0:3], func=AF.Copy)

    for ti in range(T // TA):
        ts = slice(ti * TA, (ti + 1) * TA)
        k.dma("pool", h_b[:], src_v[:, :, ts], reads=[src.sub(ti)], writes=[h_b], owner=h_b)
        if full:
            k.dma("sp", h_f[:], src_v[:, :, ts], reads=[src.sub(ti)], writes=[h_f.sub(d) for d in range(8)], owner=h_f)
        c.pool = list(range(8))
        chunks = list(range(18)) if full else list(range(6, 18))

        def stage_a(cc):
            xc = xcs[cc % 3]
            p = proj_chunk(c, w_in, w_in, cc * 128, h_b, TA)
            k.op("pool", "tensor_copy", [halo.sub(cc)], [xc], out=xc[:, 0:3], in_=halo[:, cc, :])
            k.op("act", "activation", [p], [xc], out=xc[:, 3:TA + 3], in_=p[:, 0:TA], func=AF.Copy)
            k.op("pool", "tensor_copy", [xc], [halo.sub(cc)], out=halo[:, cc, :], in_=xc[:, TA:TA + 3])

        def stage_b(cc):
            xc = xcs[cc % 3]
            p2 = next_ps(c)
            for j in range(4):
                k.op("pe", "matmul", [dconv, xc], [p2], p2[:, 0:TA], dconv[:, cc, j, :], xc[:, j:j + TA], start=(j == 0), stop=(j == 3))
            if cc >= 12:
                k.op("act", "activation", [p2], [qkv.sub(cc)], out=qkv[:, cc, :], in_=p2[:, 0:TA], func=AF.Silu)
            else:
                k.op("act", "activation", [p2], [ystore.sub(cc)], out=ystore[:, cc, :], in_=p2[:, 0:TA], func=AF.Silu)

        for i_, cc in enumerate(chunks + [None]):
            if cc is not None:
                stage_a(cc)
            if i_ >= 1:
                stage_b(chunks[i_ - 1])
        if full:
            for hh in range(6):
                p = proj_chunk(c, w_in, w_in, C_Z + hh * 128, h_b, TA)
                k.op("act", "activation", [p], [zs.sub(hh)], out=zs[:, hh, :], in_=p[:, 0:TA], func=AF.Silu)
            for cc in range(2):
                p = proj_chunk(c, w_in, w_in, C_QM + cc * 128, h_b, TA)
                k.op("act", "activation", [p], [qmT.sub(cc)], out=qmT[:, cc, :], in_=p[:, 0:TA], func=AF.Copy)
        for cc in [x_ for x_ in chunks if x_ < 12]:
            sq, r_ = sqb[cc % 2], rn[cc % 2]
            k.op("act", "activation", [ystore.sub(cc)], [sq], out=sq[:], in_=ystore[:, cc, :], func=AF.Square)
            p3 = next_ps(c)
            k.op("pe", "matmul", [ones_b, sq], [p3], p3[:, 0:TA], ones_b[:], sq[:], start=True, stop=True)
            k.op("act", "activation", [p3, small], [r_], out=r_[:], in_=p3[:, 0:TA], func=AF.Ln, bias=small[:, 18:19])
            k.op("act", "activation", [r_, small], [r_], out=r_[:], in_=r_[:], func=AF.Exp, scale=-0.5,
                 bias=(small[:, 20:21] if cc < 6 else small[:, 21:22]))
            k.op("dve", "tensor_tensor", [ystore.sub(cc), r_], [qkv.sub(cc)], out=qkv[:, cc, :], in0=ystore[:, cc, :], in1=r_[:], op=ALU.mult)
        p = next_ps(c)
        for blk in range(NBK):
            for kk in range(8):
                k.op("pe", "matmul", [w_in, h_b], [p], p[:, blk * 12:(blk + 1) * 12], h_b[:, kk, blk * 128:(blk + 1) * 128],
                     w_in[:, kk, C_AB:C_AB + 12], start=(kk == 0), stop=(kk == 7))
        pv = p[:, 0:12 * NBK].rearrange("p (b x) -> p b x", b=NBK)
        k.op("dve", "tensor_tensor", [p, small], [gt], out=gt[:, :, 18:30], in0=pv, in1=small[:, 0:12].unsqueeze(1).to_broadcast([128, NBK, 12]),
             op=ALU.add)
        k.op("act", "activation", [gt], [gt], out=gt[:, :, 0:6], in_=gt[:, :, 18:24], func=AF.Exp)
        k.op("act", "activation", [gt, small], [gt], out=gt[:, :, 0:6], in_=gt[:, :, 0:6], func=AF.Ln, bias=small[:, 19:20])
        k.op("dve", "tensor_tensor", [gt, small], [gt], out=gt[:, :, 0:6], in0=gt[:, :, 0:6],
             in1=small[:, 12:18].unsqueeze(1).to_broadcast([128, NBK, 6]), op=ALU.mult)
        k.op("act", "activation", [gt], [gt], out=gt[:, :, 12:18], in_=gt[:, :, 24:30], func=AF.Exp, scale=-1.0)
        k.op("act", "activation", [gt, small], [gt], out=gt[:, :, 12:18], in_=gt[:, :, 12:18], func=AF.Ln, bias=small[:, 19:20])
        k.op("dve", "tensor_scalar", [gt], [gt], out=gt[:, :, 12:18], in0=gt[:, :, 12:18], scalar1=-1.0, scalar2=None, op0=ALU.mult)
        k.op("act", "activation", [gt], [gt], out=gt[:, :, 6:12], in_=gt[:, :, 12:18], func=AF.Exp)

        c.pool = [0, 1]
        for blk in range(NBK):
            bs = slice(blk * 128, (blk + 1) * 128)
            p = next_ps(c)
            k.op("pe", "matmul", [cf["U"], gt], [p], p[:, 0:6], cf["U"][:], gt[:, blk, 0:6], start=True, stop=True)
            k.op("pe", "matmul", [cf["CH"], gt], [p], p[:, 6:12], cf["CH"][:], gt[:, blk, 0:6], start=True, stop=True)
            k.op("dve", "tensor_copy", [p], [sv], out=sv[:, 0:6], in_=p[:, 0:6])
            k.op("dve", "tensor_copy", [p], [sv], out=sv[:, 30:36], in_=p[:, 6:12])
            k.op("dve", "tensor_scalar", [sv], [sv], out=sv[:, 6:12], in0=sv[:, 0:6], scalar1=-1.0, scalar2=None, op0=ALU.mult)
            k.op("dve", "tensor_tensor", [sv, gt], [sv], out=sv[:, 12:18], in0=sv[:, 0:6], in1=gt[:, blk, 12:18], op=ALU.add)
            k.op("act", "activation", [sv], [sv], out=sv[:, 18:24], in_=sv[:, 12:18], func=AF.Exp)
            k.op("dve", "tensor_tensor", [sv], [sv], out=sv[:, 24:30], in0=sv[:, 30:36], in1=sv[:, 0:6], op=ALU.subtract)
            k.op("act", "activation", [sv], [sv], out=sv[:, 24:30], in_=sv[:, 24:30], func=AF.Exp)
            k.op("dve", "tensor_tensor", [ident_f, sv], [Yf.sub(0), Yf.sub(1)], out=Dg[:], in0=ident_f[:].unsqueeze(1).to_broadcast([128, NH, 128]),
                 in1=sv[:, 0:6].unsqueeze(2).to_broadcast([128, NH, 128]), op=ALU.mult)
            for g in range(2):
                hs = slice(3 * g, 3 * g + 3)
                Dg3 = Dg[:, hs, :].rearrange("p h j -> p (h j)")
                for bank, mk in (((PB[2], "ML3"), (PB[3], "MA3"), (PB[4], None)) if full else ((PB[2], "ML3"), (PB[4], None))):
                    k.op("pe", "matmul", [ones1, Yf.sub(g)], [bank], bank[:, 0:384], ones1[:], Dg3, start=True, stop=(mk is None))
                    if mk is not None:
                        k.op("pe", "matmul", [ident_f, cf[mk]], [bank], bank[:, 0:384], ident_f[:], cf[mk][:], start=False, stop=True)
                for hh in range(3):
                    h = 3 * g + hh
                    k.op("act", "activation", [PB[2], sv], [Eb.sub(h)], out=Eb[:, h, :], in_=PB[2][:, hh * 128:(hh + 1) * 128], func=AF.Exp,
                         scale=-1.0, bias=sv[:, 12 + h:13 + h])
                    if full:
                        k.op("act", "activation", [PB[3], sv], [ET.sub(h)], out=ET[:, h, :], in_=PB[3][:, hh * 128:(hh + 1) * 128],
                             func=AF.Exp, scale=1.0, bias=sv[:, 6 + h:7 + h])
                k.op("act", "activation", [PB[4]], [eGr.sub(("g", g))], out=eGr[:, hs, :].rearrange("p h j -> p (h j)"),
                     in_=PB[4][:, 0:384], func=AF.Exp)
            EbR = [Eb.sub(h) for h in range(NH)]
            ETR = [ET.sub(h) for h in range(NH)]
            eGR = [eGr.sub(("g", g)) for g in range(2)]
            kq = qkv
            for g in range(2):
                hs = slice(3 * g, 3 * g + 3)
                for hh in range(3):
                    h = 3 * g + hh
                    k.op("pe", "matmul", [kq.sub(6 + h)], [PB[5]], PB[5][:, hh * 128:(hh + 1) * 128], kq[:, 6 + h, bs], kq[:, 6 + h, bs],
                         start=True, stop=True)
                for hh in range(3):
                    h = 3 * g + hh
                    if full:
                        k.op("pe", "matmul", [kq.sub(6 + h), kq.sub(h)], [PB[6]], PB[6][:, hh * 128:(hh + 1) * 128], kq[:, 6 + h, bs],
                             kq[:, h, bs], start=True, stop=True)
                k.op("dve", "tensor_tensor", [PB[5]] + EbR[3 * g:3 * g + 3], [Lc[0].sub(g)], out=Lc[0][:, hs, :].rearrange("p h j -> p (h j)"),
                     in0=PB[5][:, 0:384], in1=Eb[:, hs, :].rearrange("p h j -> p (h j)"), op=ALU.mult)
                if full:
                    k.op("dve", "tensor_tensor", [PB[6]] + ETR[3 * g:3 * g + 3], [ATb.sub(g)], out=ATb[:, hs, :].rearrange("p h j -> p (h j)"),
                         in0=PB[6][:, 0:384], in1=ET[:, hs, :].rearrange("p h j -> p (h j)"), op=ALU.mult)
            tp = PB[7]
            tpb = tp[:].bitcast(BF16)
            for h in range(NH):
                k.op("pe", "transpose", [Lc[0].sub(h // 3), c.ident_b], [tp], tpb[:, h * 128:(h + 1) * 128], Lc[0][:, h, :], c.ident_b[:])
            k.op("act", "activation", [tp], [Nc[0].sub(0), Nc[0].sub(1)], out=Nc[0][:].rearrange("p h j -> p (h j)"), in_=tpb[:, 0:768],
                 func=AF.Copy)
            for g in range(2):
                hs = slice(3 * g, 3 * g + 3)
                k.op("dve", "tensor_tensor", [cf["ident3"], Nc[0].sub(g)], [Yf.sub(g)], out=Yf[:, hs, :].rearrange("p h j -> p (h j)"),
                     in0=cf["ident3"][:], in1=Nc[0][:, hs, :].rearrange("p h j -> p (h j)"), op=ALU.subtract)
                k.op("act", "activation", [Yf.sub(g)], [Yb.sub(g)], out=Yb[:, hs, :], in_=Yf[:, hs, :], func=AF.Copy)
            def _sq(lvl):
                cur, nxt = (lvl - 1) % 2, lvl % 2
                for g in range(2):
                    hs = slice(3 * g, 3 * g + 3)
                    bL, bN = (PB[5], PB[6]) if g == 0 else (PB[2], PB[3])
                    for hh in range(3):
                        h = 3 * g + hh
                        k.op("pe", "matmul", [Nc[cur].sub(g), Lc[cur].sub(g)], [bL], bL[:, hh * 128:(hh + 1) * 128], Nc[cur][:, h, :],
                             Lc[cur][:, h, :], start=True, stop=True)
                    k.op("act", "activation", [bL], [Lc[nxt].sub(g)], out=Lc[nxt][:, hs, :].rearrange("p h j -> p (h j)"),
                         in_=bL[:, 0:384], func=AF.Copy)
                    if lvl < 5:
                        for hh in range(3):
                            h = 3 * g + hh
                            k.op("pe", "matmul", [Nc[cur].sub(g), Lc[cur].sub(g)], [bN], bN[:, hh * 128:(hh + 1) * 128], Lc[cur][:, h, :],
                                 Nc[cur][:, h, :], start=True, stop=True)
                        k.op("dve", "tensor_copy", [bN], [Nc[nxt].sub(g)], out=Nc[nxt][:, hs, :].rearrange("p h j -> p (h j)"),
                             in_=bN[:, 0:384])

            def _yap(lvl):
                buf = lvl % 2
                for g in range(2):
                    hs = slice(3 * g, 3 * g + 3)
                    bY = PB[7] if g == 0 else PB[4]
                    for hh in range(3):
                        h = 3 * g + hh
                        k.op("pe", "matmul", [Lc[buf].sub(g), Yb.sub(g)], [bY], bY[:, hh * 128:(hh + 1) * 128], Lc[buf][:, h, :],
                             Yb[:, h, :], start=True, stop=True)
                    k.op("dve", "tensor_tensor", [Yf.sub(g), bY], [Yf.sub(g)], out=Yf[:, hs, :].rearrange("p h j -> p (h j)"),
                         in0=Yf[:, hs, :].rearrange("p h j -> p (h j)"), in1=bY[:, 0:384], op=ALU.add)
                    k.op("act", "activation", [Yf.sub(g)], [Yb.sub(g)], out=Yb[:, hs, :], in_=Yf[:, hs, :], func=AF.Copy)

            _sq(1)
            for lvl in range(1, 6):
                if lvl < 5:
                    _sq(lvl + 1)
                _yap(lvl)
            tp2 = PB[2]
            tp2b = tp2[:].bitcast(BF16)
            for h in range(NH):
                k.op("pe", "transpose", [kq.sub(6 + h), c.ident_b], [tp2], tp2b[:, h * 128:(h + 1) * 128], kq[:, 6 + h, bs], c.ident_b[:])
            k.op("dve", "tensor_tensor", [tp2, sv], [Xk], out=Xk[:], in0=tp2b[:, 0:768].rearrange("p (h d) -> p h d", h=NH),
                 in1=sv[:, 18:24].unsqueeze(2).to_broadcast([128, NH, 128]), op=ALU.mult)
            for h in range(NH):
                k.op("act", "activation", [tp2, sv], [kd], out=kd[:, h, :], in_=tp2b[:, h * 128:(h + 1) * 128], func=AF.Copy,
                     scale=sv[:, 24 + h:25 + h])
            tp3 = PB[3]
            tp3b = tp3[:].bitcast(BF16)
            for h in range(NH):
                k.op("pe", "transpose", [kq.sub(12 + h), c.ident_b], [tp3], tp3b[:, h * 128:(h + 1) * 128], kq[:, 12 + h, bs], c.ident_b[:])
            k.op("dve", "tensor_tensor", [tp3, gt], [Xv], out=Xv[:], in0=tp3b[:, 0:768].rearrange("p (h d) -> p h d", h=NH),
                 in1=gt[:, blk, 6:12].unsqueeze(2).to_broadcast([128, NH, 128]), op=ALU.mult)
            for g in range(2):
                hs = slice(3 * g, 3 * g + 3)
                for hh in range(3):
                    h = 3 * g + hh
                    k.op("pe", "matmul", [Yb.sub(g), Xv], [PB[5]], PB[5][:, hh * 128:(hh + 1) * 128], Yb[:, h, :], Xv[:, h, :],
                         start=True, stop=True)
                k.op("act", "activation", [PB[5]], [u_f.sub(g)], out=u_f[:, hs, :].rearrange("p h j -> p (h j)"), in_=PB[5][:, 0:384],
                     func=AF.Copy)
                for hh in range(3):
                    h = 3 * g + hh
                    k.op("pe", "matmul", [Yb.sub(g), Xk], [PB[6]], PB[6][:, hh * 128:(hh + 1) * 128], Xk[:, h, :], Yb[:, h, :],
                         start=True, stop=True)
                k.op("dve", "tensor_copy", [PB[6]], [wT.sub(g)], out=wT[:, hs, :].rearrange("p h j -> p (h j)"), in_=PB[6][:, 0:384])
            if full:
                for g in range(2):
                    hs = slice(3 * g, 3 * g + 3)
                    k.op("dve", "tensor_tensor", [kq.sub(3 * g), kq.sub(3 * g + 1), kq.sub(3 * g + 2), eGR[g]], [qdT.sub(g)],
                         out=qdT[:, hs, :], in0=kq[:, hs, bs], in1=eGr[:, hs, :], op=ALU.mult)
            for ch in range(2):
                cs = slice(ch * 64, ch * 64 + 64)
                last = ch * 64 + 63
                vb = [PB[2], PB[3], PB[4]] if not full else [PB[2], PB[3]]
                per = 2 if not full else 3
                for h in range(NH):
                    bank = vb[h // per]
                    o_ = (h % per) * NV
                    k.op("pe", "matmul", [wT.sub(h // 3), Sb[h]], [bank], bank[:, o_:o_ + NV], wT[:, h, :], S_b[:, h, :], start=True, stop=True)
                for h in range(NH):
                    bank = vb[h // per]
                    o_ = (h % per) * NV
                    k.op("dve", "scalar_tensor_tensor", [bank, u_f.sub(h // 3)], [vnew.sub(h)], out=vnew[cs, h, 0:128], in0=bank[cs, o_:o_ + 128],
                         scalar=-1.0, in1=u_f[cs, h, :], op0=ALU.mult, op1=ALU.add)
                    if not full:
                        k.op("act", "activation", [bank], [vnew.sub(h)], out=vnew[cs, h, 128:256], in_=bank[cs, o_ + 128:o_ + 256],
                             func=AF.Copy, scale=-1.0)
                if full:
                    ob = PB[4]
                    for h in range(NH):
                        k.op("pe", "matmul", [Sb[h], qdT.sub(h // 3)], [ob], ob[:, h * 64:(h + 1) * 64], S_b[:, h, :], qdT[:, h, cs],
                             start=True, stop=False)
                        k.op("pe", "matmul", [vnew.sub(h), ATb.sub(h // 3)], [ob], ob[:, h * 64:(h + 1) * 64], vnew[cs, h, 0:128],
                             ATb[cs, h, cs], start=False, stop=True)
                    k.op("act", "activation", [ob], [orawT.sub((blk, ch))], out=orawT[:, :, blk * 128 + ch * 64: blk * 128 + ch * 64 + 64],
                         in_=ob[:, 0:384].rearrange("p (h t) -> p h t", h=NH), func=AF.Copy)
                db = [PB[5], PB[6], PB[7]] if not full else [PB[5], PB[6]]
                for h in range(NH):
                    bank = db[h // per]
                    o_ = (h % per) * NV
                    k.op("pe", "matmul", [kd, vnew.sub(h)], [bank], bank[:, o_:o_ + NV], kd[cs, h, :], vnew[cs, h, :], start=True, stop=True)
                for h in range(NH):
                    bank = db[h // per]
                    o_ = (h % per) * NV
                    k.op("dve", "scalar_tensor_tensor", [Sf[h], eGR[h // 3], bank], [Sf[h]], out=S_f[:, h, :], in0=S_f[:, h, :],
                         scalar=eGr[:, h, last:last + 1], in1=bank[:, o_:o_ + NV], op0=ALU.mult, op1=ALU.add)
                    k.op("act", "activation", [Sf[h]], [Sb[h]], out=S_b[:, h, :], in_=S_f[:, h, :], func=AF.Copy)
        if not full:
            continue
        for h in range(NH):
            y, sq, r_ = ysil[h % 2], sqb[h % 2], rn[h % 2]
            rd = [orawT.sub((b_, c_)) for b_ in range(NBK) for c_ in range(2)]
            k.op("act", "activation", rd, [sq], out=sq[:], in_=orawT[:, h, :], func=AF.Square)
            p3 = next_ps(c)
            k.op("pe", "matmul", [ones_b, sq], [p3], p3[:, 0:TA], ones_b[:], sq[:], start=True, stop=True)
            k.op("act", "activation", [p3, small], [r_], out=r_[:], in_=p3[:, 0:TA], func=AF.Ln, bias=small[:, 18:19], scale=1.0 / 128)
            k.op("act", "activation", [r_], [r_], out=r_[:], in_=r_[:], func=AF.Exp, scale=-0.5)
            k.op("dve", "tensor_tensor", rd + [r_], [y], out=y[:], in0=orawT[:, h, :], in1=r_[:], op=ALU.mult)
            k.op("dve", "scalar_tensor_tensor", [y, normw, zs.sub(h)], [oT.sub((h, 0, 0))], out=oT[:, h, :], in0=y[:], scalar=normw[:, 0:1],
                 in1=zs[:, h, :], op0=ALU.mult, op1=ALU.mult)
        items = []
        for blk in range(NBK):
            qs = slice(blk * 128, (blk + 1) * 128)
            for hd in range(4):
                cc, pb = hd // 2, (hd % 2) * 64
                items.append(dict(q=(qmT.sub(cc), qmT[pb:pb + 64, cc, qs]),
                                  kT=(memKT, memKT[pb:pb + 64, cc, :]),
                                  v=(memV, [memV[:, kb, cc * 128:(cc + 1) * 128] for kb in range(2)]),
                                  mask=None, sink=None, scale=0.125,
                                  out=(oT.sub((6 + cc, blk, pb)), oT[pb:pb + 64, 6 + cc, qs]), pb=pb))
        attn_run(A, items)
        oT_reads = [oT.sub((h, 0, 0)) for h in range(6)] + [oT.sub((6 + cc, blk, pb)) for cc in range(2) for blk in range(4) for pb in (0, 64)]
        r = h_f
        for d in range(8):
            p = next_ps(c)
            for cc in range(8):
                rd = [w_o] + (oT_reads if (cc == 0 and d == 0) or (cc == 7 and d == 7) else [])
                k.op("pe", "matmul", rd, [p], p[:, 0:TA], w_o[:, cc, d * 128:(d + 1) * 128], oT[:, cc, :], start=(cc == 0), stop=(cc == 7))
            k.op("dve", "scalar_tensor_tensor", [h_f.sub(d), p], [r.sub(d)], out=r[:, d, :], in0=h_f[:, d, :], scalar=ALPHA,
                 in1=p[:, 0:TA], op0=ALU.mult, op1=ALU.add)
        ln_tile(c, r, lnp[:, li, 0, 0, :], lnp[:, li, 0, 1, :], tmp, lnp)
        k.dma("sp", dst_v[:, :, ts], r[:], reads=[r.sub(d) for d in range(8)], writes=[dst.sub(ti)], owner=r)
    if not full:
        k.dma("sp", L["pq_out"][:, :, :], S_f[:], reads=Sf, writes=[L["pq_out"]], owner=S_f)
    k.pop()
    c.pool = [0, 1, 2, 3]


def perm_w_in_a(w):
    out = np.zeros((w.shape[0], A_COLS), np.float32)
    out[:, 0:3072] = w[:, 0:3072]
    out[:, C_QM:C_QM + 256] = w[:, 3084:3340]
    out[:, C_AB:C_AB + 12] = w[:, 3072:3084]
    return out


def a_layer_inputs(inp, li):
    d = {}
    d["w_in"] = perm_w_in_a(inp["a_w_in"][li])
    cw = inp["a_conv_w"][li]
    d["convw"] = np.ascontiguousarray(cw.reshape(4, 18, 128).transpose(2, 1, 0))
    gc = np.concatenate([inp["a_dt_bias"][li], np.zeros(6, np.float32), inp["a_A_log"][li]]).astype(np.float32)
    d["gate_c"] = np.tile(gc[None, :], (128, 1))
    d["normw"] = np.ascontiguousarray(inp["a_norm_w"][li][:, None])
    d["w_o"] = inp["w_o"][li]
    d["mem_w_kv"] = inp["mem_w_kv"][li]
    return d


def build_a(T, li, mode):
    nc = bass.Bass("TRN2", target_bir_lowering=False)
    k = K(nc)
    inp = lambda n, s, dt=F32: k.dram(n, s, dt, "ExternalInput")
    hT = inp("hT", [D, T])
    G = {}
    cd = {"ident": inp("c_ident", [128, 128])}
    G["c_ident"] = cd["ident"]
    for n, a in make_consts_a().items():
        G["ca_" + n] = inp("ca_" + n, list(a.shape))
    G["memT"] = inp("memT", [D, 256])
    lnp_d = inp("lnp", [128, 4 * 2 * 2 * 8])
    L = dict(w_in=inp("w_in", [D, A_COLS]), convw=inp("convw", [128, 18, 4]), gate_c=inp("gate_c", [128, 18]),
             normw=inp("normw", [128, 1]), w_o=inp("w_o", [D, D]), mem_w_kv=inp("mem_w_kv", [D, 512]),
             hhalo=inp("hhalo", [D, 3]))
    c = setup_common(k, T, cd)
    lnp = k.sbuf("lnp_sb", [128, 4, 2, 2, 8], F32)
    k.dma("sp", lnp[:], lnp_d.t.rearrange("p (a b c d) -> p a b c d", a=4, b=2, c=2), writes=[lnp])
    G["lnp"] = lnp
    if mode == "full":
        pa = inp("pq_all", [8 * 128, 1536])
        L["pq_all_buf"] = pa
        L["pq_all"] = lambda r_, pa=pa: pa.t[r_ * 128:(r_ + 1) * 128, :].rearrange("p (h n) -> p h n", h=6)
        L["fold_ranks"] = [0, 1, 2, 4, 5, 6]
        L["gate_col"] = lambda r_: slice(r_, r_ + 1)
        gd = inp("gates", [128, 16])
        G["gates"] = k.sbuf("gates_sb", [128, 16], F32)
        k.dma("sp", G["gates"][:], gd[:, :], writes=[G["gates"]])
        outT = k.dram("outT", [D, T], F32, "ExternalOutput")
        mixer_a_phase(c, hT, outT, G, L, li, "full")
        k.out_events.append(list(k.all_dma.values())[-1])
    else:
        L["pq_out"] = k.dram("pq", [128, 6, 256], F32, "ExternalOutput")
        mixer_a_phase(c, hT, None, G, L, li, "pq")
    k.barrier()
    k.finish()
    return nc


NSEG = 4


def shared_kv_phase(c, src, G, L):
    k = c.k
    k.push()
    wkv = k.sbuf("wkvs", [128, 8, 512], BF16)
    load_w(k, wkv, L["w_kv_dup"])
    h_b = k.sbuf("h_b", [128, 8, TT], BF16)
    pos_i = k.sbuf("pos_i", [128, TT], I32)
    cos_t = k.sbuf("cos_t", [128, TT], F32)
    sin_t = k.sbuf("sin_t", [128, TT], F32)
    rt = {"ang": k.sbuf("ang", [128, TT], F32), "kf": k.sbuf("kf", [128, TT], F32), "ki": k.sbuf("ki", [128, TT], I32),
          "gg": k.sbuf("gg", [128, TT], F32)}
    rs = {"qa": [k.sbuf(f"qa{i}", [128, TT], BF16) for i in range(2)],
          "t1": [rt["ang"], rt["kf"]], "t2": [rt["gg"], k.sbuf("t2b", [128, TT], F32)]}
    ko = k.sbuf("ko", [128, 2, TT], F32)
    vo = k.sbuf("vo", [128, 4, 256], F32)
    src_v = src.t.rearrange("(k p) t -> p k t", p=128)
    vsh_v = L["vsh"].t.rearrange("(b p) x -> p b x", p=128)
    for ti in range(c.NT):
        ts = slice(ti * TT, (ti + 1) * TT)
        k.dma("pool", h_b[:], src_v[:, :, ts], reads=[src.sub(ti)], writes=[h_b], owner=h_b)
        k.dma("sp", pos_i[:], G["pos"][:, ts], writes=[pos_i])
        rope_tables(c, pos_i, G["invf"], cos_t, sin_t, rt)
        for kv in range(2):
            p = proj_chunk(c, wkv, wkv, kv * 128, h_b)
            rope_chunk(c, p, G["rot_b"], cos_t, sin_t, rs, ko.sub(kv), ko[:, kv, :], kv)
        k.dma("sp", L["kshT"][:, :, ts], ko[:], reads=[ko.sub(0), ko.sub(1)], writes=[L["kshT"].sub(ti)], owner=ko, is_output=True)
        for blk in range(4):
            p = next_ps(c)
            for kk in range(8):
                k.op("pe", "matmul", [wkv, h_b], [p], p[:, 0:256], h_b[:, kk, blk * 128:(blk + 1) * 128], wkv[:, kk, 256:512],
                     start=(kk == 0), stop=(kk == 7))
            k.op("act", "activation", [p], [vo.sub(blk)], out=vo[:, blk, :], in_=p[:, 0:256], func=AF.Copy)
        k.dma("sp", vsh_v[:, ti * 4:(ti + 1) * 4, :], vo[:], reads=[vo.sub(b_) for b_ in range(4)], writes=[L["vsh"].sub(ti)], owner=vo,
              is_output=True)
    k.pop()


def _a_inputs(k, T, full):
    inp = lambda n, s, dt=F32: k.dram(n, s, dt, "ExternalInput")
    G = {}
    cd = {"ident": inp("c_ident", [128, 128])}
    G["c_ident"] = cd["ident"]
    for n, a in make_consts_a().items():
        G["ca_" + n] = inp("ca_" + n, list(a.shape))
    L = dict(w_in=inp("w_in", [D, A_COLS]), convw=inp("convw", [128, 18, 4]), gate_c=inp("gate_c", [128, 18]),
             hhalo=inp("hhalo", [D, 3]))
    if full:
        G["memT"] = inp("memT", [D, 256])
        L.update(normw=inp("normw", [128, 1]), w_o=inp("w_o", [D, D]), mem_w_kv=inp("mem_w_kv", [D, 512]))
        pa = inp("pq_all", [8 * 128, 1536])
        L["pq_all_buf"] = pa
        L["pq_all"] = lambda r_, pa=pa: pa.t[r_ * 128:(r_ + 1) * 128, :].rearrange("p (h n) -> p h n", h=6)
        L["fold_ranks"] = [0, 1, 2, 4, 5, 6]
        L["gate_col"] = lambda r_: slice(r_, r_ + 1)
        G["gates_d"] = inp("gates", [128, 16])
    return G, L, cd


def build_pq(T, li):
    nc = bass.Bass("TRN2", target_bir_lowering=False)
    k = K(nc)
    hT = k.dram("hT", [D, T], F32, "ExternalInput")
    G, L, cd = _a_inputs(k, T, False)
    c = setup_common(k, T, cd)
    L["pq_out"] = k.dram("pq", [128, 6, 256], F32, "ExternalOutput")
    mixer_a_phase(c, hT, None, G, L, li, "pq")
    k.finish()
    return nc


def build_full(T, li, with_kv):
    nc = bass.Bass("TRN2", target_bir_lowering=False)
    k = K(nc)
    inp = lambda n, s, dt=F32: k.dram(n, s, dt, "ExternalInput")
    hT = inp("hT", [D, T])
    G, L, cd = _a_inputs(k, T, True)
    lnp_d = inp("lnp", [128, 4 * 2 * 2 * 8])
    w_up, w_dn = inp("w_up", [D, HID]), inp("w_dn", [HID, D])
    c = setup_common(k, T, cd)
    lnp = k.sbuf("lnp_sb", [128, 4, 2, 2, 8], F32)
    k.dma("sp", lnp[:], lnp_d.t.rearrange("p (a b c d) -> p a b c d", a=4, b=2, c=2), writes=[lnp])
    G["lnp"] = lnp
    G["gates"] = k.sbuf("gates_sb", [128, 16], F32)
    k.dma("sp", G["gates"][:], G["gates_d"][:, :], writes=[G["gates"]])
    hmid = k.dram("hmid", [D, T], F32, "Internal")
    outT = k.dram("outT", [D, T], F32, "ExternalOutput")
    mixer_a_phase(c, hT, hmid, G, L, li, "full")
    mlp_phase(c, hmid, outT, w_up, w_dn, lnp, li, dst_is_output=True)
    if with_kv:
        G["pos"] = inp("pos", [128, T], I32)
        cr, ci = inp("c_rot", [128, 128]), inp("c_invf", [128, 1])
        G["rot_b"] = k.sbuf("rot_b", [128, 128], BF16)
        k.dma("pool", G["rot_b"][:], cr[:, :], writes=[G["rot_b"]])
        G["invf"] = k.sbuf("invf", [128, 1], F32)
        k.dma("sp", G["invf"][:], ci[:, :], writes=[G["invf"]])
        L["w_kv_dup"] = inp("w_kv_dup", [D, 512])
        L["kshT"] = k.dram("kshT", [128, 2, T], F32, "ExternalOutput")
        L["vsh"] = k.dram("vsh", [T, 256], F32, "ExternalOutput")
        shared_kv_phase(c, outT, G, L)
    k.finish()
    return nc


def _lnp_host(inp):
    lnp = np.stack([np.asarray(inp["ln_g"]), np.asarray(inp["ln_b"])], axis=2)
    return np.ascontiguousarray(lnp.reshape(4, 2, 2, 8, 128).transpose(4, 0, 1, 2, 3).reshape(128, -1)).astype(np.float32)


def kernel(**inp):
    inp = {k_: np.asarray(v_) for k_, v_ in inp.items()}
    x, mem, positions = inp["x"], inp["mem"], inp["positions"]
    B, S, _ = x.shape
    T = S // NSEG
    NC = B * NSEG
    cores = list(range(NC))
    bs = [(c_ // NSEG, c_ % NSEG) for c_ in cores]
    ca = make_consts_a()
    cb = make_consts()
    lnp = _lnp_host(inp)
    ident = np.eye(128, dtype=np.float32)
    memT = [np.ascontiguousarray(mem[b].T) for b in range(B)]
    zeros_pq = np.zeros((128, 6, 256), np.float32)

    def halo3(hTs, c_):
        b, s = bs[c_]
        return np.zeros((D, 3), np.float32) if s == 0 else np.ascontiguousarray(hTs[c_ - 1][:, -3:])

    def a_common(c_, li, hTs):
        d = {"hT": hTs[c_], "hhalo": halo3(hTs, c_), "c_ident": ident}
        for n, a in ca.items():
            d["ca_" + n] = a
        al = a_layer_inputs(inp, li)
        for n in ("w_in", "convw", "gate_c"):
            d[n] = al[n]
        return d, al

    hTs = [np.ascontiguousarray(x[b, s * T:(s + 1) * T, :].T) for (b, s) in bs]
    kv = None
    for li in range(2):
        nc1 = build_pq(T, li)
        maps = [a_common(c_, li, hTs)[0] for c_ in cores]
        r1 = run_bass_kernel_spmd(nc1, maps, core_ids=cores)
        pqs = [r1.results[c_]["pq"] for c_ in cores]
        pq_all = np.zeros((8 * 128, 1536), np.float32)
        for c_ in cores:
            pq_all[c_ * 128:(c_ + 1) * 128] = pqs[c_].reshape(128, 1536)
        nc2 = build_full(T, li, with_kv=(li == 1))
        maps = []
        for c_ in cores:
            b, s = bs[c_]
            d, al = a_common(c_, li, hTs)
            d.update(memT=memT[b], normw=al["normw"], w_o=al["w_o"], mem_w_kv=al["mem_w_kv"], lnp=lnp,
                     w_up=inp["mlp_w_up"][li], w_dn=inp["mlp_w_down"][li])
            d["pq_all"] = pq_all
            g = np.zeros((128, 16), np.float32)
            for r_ in cores:
                rb, rs_ = bs[r_]
                if rb == b and rs_ < s:
                    g[:, r_] = 1.0
            d["gates"] = g
            if li == 1:
                wk = inp["w_kv_shared"]
                Kc, Vc = wk[:, 0:128], wk[:, 128:256]
                dup = lambda m: np.concatenate([m[:, 0:64], m[:, 0:64], m[:, 64:128], m[:, 64:128]], axis=1)
                d["w_kv_dup"] = np.ascontiguousarray(np.concatenate([dup(Kc), dup(Vc)], axis=1))
                d["pos"] = np.ascontiguousarray(np.tile(positions[b, s * T:(s + 1) * T][None, :], (128, 1))).astype(np.int32)
                d["c_rot"], d["c_invf"] = cb["rot"], cb["invf"]
            maps.append(d)
        r2 = run_bass_kernel_spmd(nc2, maps, core_ids=cores)
        hTs = [r2.results[c_]["outT"] for c_ in cores]
        if li == 1:
            kv = [(r2.results[c_]["kshT"], r2.results[c_]["vsh"]) for c_ in cores]
    nc5 = build_b(T)
    maps = []
    for c_ in cores:
        b, s = bs[c_]
        kT = np.zeros((128, 2, 128 + T), np.float32)
        v = np.zeros((128 + T, 2, 128), np.float32)
        kT[:, :, 128:] = kv[c_][0]
        v[128:] = kv[c_][1].reshape(T, 2, 128)
        if s > 0:
            kT[:, :, :128] = kv[c_ - 1][0][:, :, -128:]
            v[:128] = kv[c_ - 1][1][-128:].reshape(128, 2, 128)
        d = {"hT": hTs[c_], "pos": np.ascontiguousarray(np.tile(positions[b, s * T:(s + 1) * T][None, :], (128, 1))).astype(np.int32),
             "kT": kT, "v": v, "memT": memT[b], "sinks": np.tile(inp["b_sinks"].reshape(1, 24), (128, 1)).astype(np.float32), "lnp": lnp}
        for n, a in cb.items():
            d["c_" + n] = a
        if s > 0:
            d["c_mask_first"] = cb["mask_band"]
        for li in (2, 3):
            d[f"w_in{li}"] = inp["b_w_in"][li - 2]
            d[f"w_o{li}"] = inp["w_o"][li]
            d[f"mwkv{li}"] = inp["mem_w_kv"][li]
            d[f"w_up{li}"] = inp["mlp_w_up"][li]
            d[f"w_dn{li}"] = inp["mlp_w_down"][li]
        maps.append(d)
    r5 = run_bass_kernel_spmd(nc5, maps, core_ids=cores)
    out = np.zeros((B, S, D), np.float32)
    for c_ in cores:
        b, s = bs[c_]
        out[b, s * T:(s + 1) * T, :] = r5.results[c_]["outT"].T
    return out
```
